# Optimizing a Trainium2 kernel written in Bass

```python
import math
import jax, jax.numpy as jnp
from jax import lax
import numpy as np

D_MODEL = 1024
BATCH = 8
SEQ = 8192
DEPTH = 2
DEC_BATCH = 8
DEC_SEQ = 32
PAST_LEN = 1024

CHUNK = 64
N_META = 16
N_HEADS = 8
HEAD_DIM = D_MODEL // 16
D_ATTN = N_HEADS * HEAD_DIM
D_POOL = D_MODEL // 2
POOL_WINDOWS = (2, 4, 8, 16)
N_POOL_GROUPS = len(POOL_WINDOWS)
POOL_GROUP = D_POOL // N_POOL_GROUPS
POOL_STATE = max(POOL_WINDOWS) - 1
D_FF = 4 * D_MODEL
Q_BLOCK = 128
ALPHA = (2.0 * DEPTH) ** 0.25
BETA = (8.0 * DEPTH) ** -0.25
LN_EPS = 1e-5
FORGET_BIAS_INIT = 3.0

OFF_Q = 0
OFF_K = D_ATTN
OFF_V = 2 * D_ATTN
OFF_F = 3 * D_ATTN
OFF_U = OFF_F + N_HEADS
OFF_GA = OFF_U + D_POOL
OFF_GP = OFF_GA + D_MODEL
D_IN = OFF_GP + D_MODEL

kernel_name = "fox_pool_gated_streaming_encoder"


def layer_norm(x, g, b):
    xf = x.astype(jnp.float32)
    mu = jnp.mean(xf, axis=-1, keepdims=True)
    var = jnp.mean(jnp.square(xf - mu), axis=-1, keepdims=True)
    return ((xf - mu) * lax.rsqrt(var + LN_EPS) * g + b).astype(x.dtype)


def in_proj(x, w_in, b_f):
    B, L = x.shape[0], x.shape[1]
    z = jnp.einsum('bld,de->ble', x, w_in)
    q = z[..., OFF_Q:OFF_K].reshape(B, L, N_HEADS, HEAD_DIM)
    k = z[..., OFF_K:OFF_V].reshape(B, L, N_HEADS, HEAD_DIM)
    v = z[..., OFF_V:OFF_F].reshape(B, L, N_HEADS, HEAD_DIM)
    logf = jax.nn.log_sigmoid((z[..., OFF_F:OFF_U] + b_f).astype(jnp.float32))
    u = z[..., OFF_U:OFF_GA]
    ga = z[..., OFF_GA:OFF_GP]
    gp = z[..., OFF_GP:]
    return q, k, v, logf, u, ga, gp


def fox_attend(q, k, v, cq, ck, qpos, kpos):
    s = jnp.einsum('bqhd,bkhd->bhqk', q, k).astype(jnp.float32) / math.sqrt(HEAD_DIM)
    s = s + jnp.transpose(cq, (0, 2, 1))[:, :, :, None] - jnp.transpose(ck, (0, 2, 1))[:, :, None, :]
    mask = kpos[None, :] <= qpos[:, None]
    s = jnp.where(mask[None, None], s, -jnp.inf)
    p = jax.nn.softmax(s, axis=-1).astype(v.dtype)
    return jnp.einsum('bhqk,bkhd->bqhd', p, v)


def fox_prompt(q, k, v, logf):
    B, L = q.shape[0], q.shape[1]
    c = jnp.cumsum(logf, axis=1)
    nb = -(-L // Q_BLOCK)
    pad = nb * Q_BLOCK - L
    qp = jnp.pad(q, ((0, 0), (0, pad), (0, 0), (0, 0)))
    cqp = jnp.pad(c, ((0, 0), (0, pad), (0, 0)))
    qb = qp.reshape(B, nb, Q_BLOCK, N_HEADS, HEAD_DIM).transpose(1, 0, 2, 3, 4)
    cqb = cqp.reshape(B, nb, Q_BLOCK, N_HEADS).transpose(1, 0, 2, 3)
    posb = jnp.arange(nb * Q_BLOCK, dtype=jnp.int32).reshape(nb, Q_BLOCK)
    kpos = jnp.arange(L, dtype=jnp.int32)
    out = lax.map(lambda a: fox_attend(a[0], k, v, a[1], c, a[2], kpos), (qb, cqb, posb))
    out = out.transpose(1, 0, 2, 3, 4).reshape(B, nb * Q_BLOCK, D_ATTN)
    return out[:, :L]


def fox_sample(q, k, v, logf, ck, cv, clf):
    B, T = q.shape[0], q.shape[1]
    P = ck.shape[1]
    kf = jnp.concatenate([ck, k], axis=1)
    vf = jnp.concatenate([cv, v], axis=1)
    c = jnp.cumsum(jnp.concatenate([clf.astype(jnp.float32), logf], axis=1), axis=1)
    qpos = P + jnp.arange(T, dtype=jnp.int32)
    kpos = jnp.arange(P + T, dtype=jnp.int32)
    out = fox_attend(q, kf, vf, c[:, P:], c, qpos, kpos)
    return out.reshape(B, T, D_ATTN)


def pool_branch(u_ext, pos, w_grp, scale):
    B = u_ext.shape[0]
    n = u_ext.shape[1] - POOL_STATE
    uf = u_ext.astype(jnp.float32)
    cs = jnp.pad(jnp.cumsum(uf, axis=1), ((0, 0), (1, 0), (0, 0)))
    cur = uf[:, POOL_STATE:]
    outs = []
    for g, w in enumerate(POOL_WINDOWS):
        sl = slice(g * POOL_GROUP, (g + 1) * POOL_GROUP)
        end = cs[:, POOL_STATE + 1:POOL_STATE + 1 + n, sl]
        start = cs[:, POOL_STATE + 1 - w:POOL_STATE + 1 - w + n, sl]
        cnt = jnp.minimum(w, pos + 1).astype(jnp.float32)[None, :, None]
        outs.append((end - start) / cnt - cur[..., sl])
    pooled = jnp.stack(outs, axis=2)
    mixed = jnp.einsum('bngc,gce->bnge', pooled.astype(w_grp.dtype), w_grp).reshape(B, n, D_POOL)
    return mixed * scale


def merge(att, pooled, ga, gp, w_attn_br, w_pool_br, w_out):
    a = jnp.einsum('bla,ad->bld', att, w_attn_br)
    p = jnp.einsum('blc,cd->bld', pooled, w_pool_br)
    m = jax.nn.sigmoid(ga) * a + jax.nn.sigmoid(gp) * p
    return jnp.einsum('bld,de->ble', m, w_out)


def sqrelu_mlp(x, w_up, w_down):
    h = jax.nn.relu(jnp.einsum('bld,df->blf', x, w_up))
    return jnp.einsum('blf,fd->bld', h * h, w_down)


def setup_inputs(seed: int = 0) -> dict:
    key = jax.random.key(seed)
    ks = jax.random.split(key, 24)
    f32 = jnp.float32
    col_scale = jnp.ones((D_IN,), f32).at[OFF_V:OFF_F].set(BETA)
    return {
        "x_prompt": jax.random.normal(ks[0], (BATCH, SEQ, D_MODEL), f32),
        "x_sample": jax.random.normal(ks[1], (DEC_BATCH, DEC_SEQ, D_MODEL), f32),
        "cache_k": jax.random.normal(ks[2], (DEPTH, DEC_BATCH, PAST_LEN, N_HEADS, HEAD_DIM), f32),
        "cache_v": jax.random.normal(ks[3], (DEPTH, DEC_BATCH, PAST_LEN, N_HEADS, HEAD_DIM), f32) * BETA,
        "cache_logf": jax.nn.log_sigmoid(FORGET_BIAS_INIT + jax.random.normal(ks[4], (DEPTH, DEC_BATCH, PAST_LEN, N_HEADS), f32)),
        "state_pool": jax.random.normal(ks[5], (DEPTH, DEC_BATCH, POOL_STATE, D_POOL), f32),
        "meta_tokens": jax.random.normal(ks[6], (N_META, D_MODEL), f32),
        "w_in": jax.random.normal(ks[7], (DEPTH, D_MODEL, D_IN), f32) * (D_MODEL ** -0.5) * col_scale,
        "b_f": FORGET_BIAS_INIT + 0.1 * jax.random.normal(ks[8], (DEPTH, N_HEADS), f32),
        "w_pool_grp": jax.random.normal(ks[9], (DEPTH, N_POOL_GROUPS, POOL_GROUP, POOL_GROUP), f32) * (POOL_GROUP ** -0.5),
        "pool_scale": 1.0 + 0.1 * jax.random.normal(ks[10], (DEPTH, D_POOL), f32),
        "w_attn_br": jax.random.normal(ks[11], (DEPTH, D_ATTN, D_MODEL), f32) * (D_ATTN ** -0.5),
        "w_pool_br": jax.random.normal(ks[12], (DEPTH, D_POOL, D_MODEL), f32) * (D_POOL ** -0.5),
        "w_out": jax.random.normal(ks[13], (DEPTH, D_MODEL, D_MODEL), f32) * (D_MODEL ** -0.5) * BETA,
        "ln1_g": 1.0 + 0.02 * jax.random.normal(ks[14], (DEPTH, D_MODEL), f32),
        "ln1_b": 0.02 * jax.random.normal(ks[15], (DEPTH, D_MODEL), f32),
        "w_up": jax.random.normal(ks[16], (DEPTH, D_MODEL, D_FF), f32) * (D_MODEL ** -0.5) * BETA,
        "w_down": jax.random.normal(ks[17], (DEPTH, D_FF, D_MODEL), f32) * (D_FF ** -0.5) * BETA,
        "ln2_g": 1.0 + 0.02 * jax.random.normal(ks[18], (DEPTH, D_MODEL), f32),
        "ln2_b": 0.02 * jax.random.normal(ks[19], (DEPTH, D_MODEL), f32),
    }


def reference(x_prompt, x_sample, cache_k, cache_v, cache_logf, state_pool, meta_tokens,
              w_in, b_f, w_pool_grp, pool_scale, w_attn_br, w_pool_br, w_out,
              ln1_g, ln1_b, w_up, w_down, ln2_g, ln2_b):
    B = x_prompt.shape[0]
    meta = jnp.broadcast_to(meta_tokens.astype(x_prompt.dtype)[None], (B, N_META, D_MODEL))
    xp = jnp.concatenate([meta, x_prompt], axis=1)
    xs = x_sample
    L = xp.shape[1]
    T = xs.shape[1]
    pos_p = jnp.arange(L, dtype=jnp.int32)
    pos_s = PAST_LEN + jnp.arange(T, dtype=jnp.int32)
    kp_l, vp_l, fp_l, pp_l = [], [], [], []
    ks_l, vs_l, fs_l, ps_l = [], [], [], []
    for l in range(DEPTH):
        q, k, v, lf, u, ga, gp = in_proj(xp, w_in[l], b_f[l])
        att = fox_prompt(q, k, v, lf)
        u_ext = jnp.concatenate([jnp.zeros((B, POOL_STATE, D_POOL), u.dtype), u], axis=1)
        pooled = pool_branch(u_ext, pos_p, w_pool_grp[l], pool_scale[l])
        mix = merge(att, pooled, ga, gp, w_attn_br[l], w_pool_br[l], w_out[l])
        xp = layer_norm(ALPHA * xp + mix, ln1_g[l], ln1_b[l])
        xp = layer_norm(ALPHA * xp + sqrelu_mlp(xp, w_up[l], w_down[l]), ln2_g[l], ln2_b[l])
        kp_l.append(k)
        vp_l.append(v)
        fp_l.append(lf)
        pp_l.append(u_ext[:, -POOL_STATE:])
        q, k, v, lf, u, ga, gp = in_proj(xs, w_in[l], b_f[l])
        att = fox_sample(q, k, v, lf, cache_k[l], cache_v[l], cache_logf[l])
        u_ext = jnp.concatenate([state_pool[l].astype(u.dtype), u], axis=1)
        pooled = pool_branch(u_ext, pos_s, w_pool_grp[l], pool_scale[l])
        mix = merge(att, pooled, ga, gp, w_attn_br[l], w_pool_br[l], w_out[l])
        xs = layer_norm(ALPHA * xs + mix, ln1_g[l], ln1_b[l])
        xs = layer_norm(ALPHA * xs + sqrelu_mlp(xs, w_up[l], w_down[l]), ln2_g[l], ln2_b[l])
        ks_l.append(k)
        vs_l.append(v)
        fs_l.append(lf)
        ps_l.append(u_ext[:, -POOL_STATE:])
    y_prompt = xp[:, N_META:]
    y_sample = xs
    k_prompt = jnp.stack(kp_l)
    v_prompt = jnp.stack(vp_l)
    logf_prompt = jnp.stack(fp_l)
    pool_prompt = jnp.stack(pp_l)
    k_sample = jnp.stack(ks_l)
    v_sample = jnp.stack(vs_l)
    logf_sample = jnp.stack(fs_l)
    pool_sample = jnp.stack(ps_l)
    return (y_prompt, y_sample, k_prompt, v_prompt, logf_prompt, pool_prompt, k_sample, v_sample, logf_sample, pool_sample)
```

```python
import os
import numpy as np
import concourse.bass as bass
import concourse.mybir as mybir
from concourse.bass_utils import run_bass_kernel_spmd

F32 = mybir.dt.float32
BF16 = mybir.dt.bfloat16
AF = mybir.ActivationFunctionType
ALU = mybir.AluOpType

D = 1024
SEQ = 8192
NMETA = 16
TS = 32
PAST = 1024
H = 8
DH = 64
DA = 512
DP = 512
DFF = 4096
DIN = 4104
OFF_Q, OFF_K, OFF_V, OFF_F, OFF_U, OFF_GA, OFF_GP = 0, 512, 1024, 1536, 1544, 2056, 3080
DEPTH = 2
ALPHA = (2.0 * DEPTH) ** 0.25
LN_EPS = 1e-5
L_TOT = NMETA + SEQ
NBLK = SEQ // 512
NCH = 32
SLAB = 162
KVW = 512 + 4 * SLAB
NW = 3
NR = 6
NP = 4


class Tok:
    __slots__ = ("sem", "key", "val", "clock")

    def __init__(self, sem, key, val, clock):
        self.sem, self.key, self.val, self.clock = sem, key, val, clock


class Eng:
    def __init__(self, nc, name, eng, same_wait):
        self.name = self.key = name
        self.eng = eng
        self.sem = nc.alloc_semaphore(name="e_" + name)
        self.count = 0
        self.seen = {}
        self.same_wait = same_wait
        self.nwaits = 0

    def wait(self, tok):
        if tok is None or self.seen.get(tok.key, 0) >= tok.val:
            return
        if not (tok.key == self.key and not self.same_wait):
            self.eng.wait_ge(tok.sem, tok.val)
            self.nwaits += 1
        self.seen[tok.key] = tok.val
        for k, v in tok.clock.items():
            if self.seen.get(k, 0) < v:
                self.seen[k] = v

    def issue(self, ins):
        self.count += 1
        ins.then_inc(self.sem, 1)
        clock = dict(self.seen)
        clock[self.key] = self.count
        return Tok(self.sem, self.key, self.count, clock)


class Lane:
    def __init__(self, nc, name):
        self.key = "l_" + name
        self.sem = nc.alloc_semaphore(name=self.key)
        self.count = 0
        self.last = None


class Buf:
    def __init__(self, name):
        self.name = name
        self.w = {}
        self.r = {}
        self.pr = {}


def _pre(eng, reads, writes, disjoint):
    for b in reads:
        for t in list(b.w.values()):
            eng.wait(t)
    for b in writes:
        for t in list(b.pr.values()):
            eng.wait(t)
        for t in list(b.r.values()):
            eng.wait(t)
        if (not disjoint) or b.r:
            for t in list(b.w.values()):
                eng.wait(t)


def _post(tok, reads, writes):
    for b in reads:
        b.r[tok.key] = tok
    for b in writes:
        if b.r:
            b.pr = b.r
            b.r = {}
            b.w = {tok.key: tok}
        else:
            b.w[tok.key] = tok


def op(eng, fns, reads=(), writes=(), disjoint=False):
    _pre(eng, reads, writes, disjoint)
    ins = None
    for f in fns:
        ins = f()
    tok = eng.issue(ins)
    _post(tok, reads, writes)
    return tok


def dma(q, lane, out, in_, reads=(), writes=(), disjoint=False, **kw):
    _pre(q, reads, writes, disjoint)
    q.wait(lane.last)
    ins = q.eng.dma_start(out=out, in_=in_, **kw)
    lane.count += 16
    ins.then_inc(lane.sem, 16)
    tok = Tok(lane.sem, lane.key, lane.count, dict(q.seen))
    lane.last = tok
    _post(tok, reads, writes)
    return tok


def build(NB=NBLK, NL=DEPTH):
    nc = bass.Bass("TRN2", target_bir_lowering=False)

    def din(name, shape):
        return nc.dram_tensor(name, list(shape), F32, kind="ExternalInput").ap()

    def dout(name, shape):
        return nc.dram_tensor(name, list(shape), F32, kind="ExternalOutput").ap()

    x_prompt = din("x_prompt", [SEQ, D])
    x_sample = din("x_sample", [TS, D])
    cache_k = din("cache_k", [DEPTH, PAST, DA])
    cache_v = din("cache_v", [DEPTH, PAST, DA])
    cache_logf = din("cache_logf", [DEPTH, PAST, H])
    state_pool = din("state_pool", [DEPTH, 15, DP])
    meta_tokens = din("meta_tokens", [NMETA, D])
    w_in = din("w_in", [DEPTH, D, DIN])
    b_f = din("b_f", [DEPTH, H])
    w_pool_grp = din("w_pool_grp", [DEPTH, 4, 128, 128])
    pool_scale = din("pool_scale", [DEPTH, DP])
    w_attn_br = din("w_attn_br", [DEPTH, DA, D])
    w_pool_br = din("w_pool_br", [DEPTH, DP, D])
    w_out = din("w_out", [DEPTH, D, D])
    ln1_g = din("ln1_g", [DEPTH, D])
    ln1_b = din("ln1_b", [DEPTH, D])
    w_up = din("w_up", [DEPTH, D, DFF])
    w_down = din("w_down", [DEPTH, DFF, D])
    ln2_g = din("ln2_g", [DEPTH, D])
    ln2_b = din("ln2_b", [DEPTH, D])
    consts = din("consts", [128, 576])

    y_prompt = dout("y_prompt", [SEQ, D])
    y_sample = dout("y_sample", [TS, D])
    k_prompt = dout("k_prompt", [DEPTH, L_TOT, DA])
    v_prompt = dout("v_prompt", [DEPTH, L_TOT, DA])
    logf_prompt = dout("logf_prompt", [DEPTH, L_TOT, H])
    pool_prompt = dout("pool_prompt", [DEPTH, 15, DP])
    k_sample = dout("k_sample", [DEPTH, TS, DA])
    v_sample = dout("v_sample", [DEPTH, TS, DA])
    logf_sample = dout("logf_sample", [DEPTH, TS, H])
    pool_sample = dout("pool_sample", [DEPTH, 15, DP])

    WS = nc.dram_tensor("ws", [DEPTH, NCH, 128, 4096], BF16, kind="Internal").ap()
    KVS = nc.dram_tensor("kvs", [DEPTH, NBLK, 128, 4, KVW], BF16, kind="Internal").ap()
    XS = nc.dram_tensor("xs", [SEQ + NMETA + TS, D], F32, kind="Internal").ap()

    PE = Eng(nc, "pe", nc.tensor, False)
    ACT = Eng(nc, "act", nc.scalar, True)
    DVE = Eng(nc, "dve", nc.vector, True)
    POOL = Eng(nc, "pool", nc.gpsimd, True)
    SP = Eng(nc, "sp", nc.sync, False)

    def sb(name, shape, dt=F32):
        return nc.alloc_sbuf_tensor(name, list(shape), dt)

    CONST = sb("const", [128, 576])
    TRIF = CONST[:, 0:128]
    ONESF = CONST[:, 128:256]
    IDF = CONST[:, 256:384]
    INVC = CONST[:, 384:448]
    SELF = CONST[:, 448:576]
    CB16 = sb("cb16", [128, 384], BF16)
    TRIB = CB16[:, 0:128]
    IDB = CB16[:, 128:256]
    SELB = CB16[:, 256:384]
    BFB = sb("bfb", [128, DEPTH * H])
    PSC = sb("psc", [128, DEPTH * 4])
    LNP = sb("lnp", [128, 4, D])
    XT = sb("xt", [128, 4, D])
    TMPN = sb("tmpn", [128, 4, D])
    XBN = sb("xbn", [128, 4, D], BF16)
    XTRA = sb("xtra", [128, 8, 512], BF16)
    XTRB = sb("xtrb", [128, 8, 512], BF16)
    QT = sb("qt", [128, 2, 4, 512], BF16)
    KVST = sb("kvst", [128, 4, KVW], BF16)
    R = sb("r", [128, 4, 4096], BF16)
    UB = sb("ub", [128, 4, 528])
    UM = sb("um", [128, 4, 32])
    US = sb("us", [128, 4, 48])
    PT1 = sb("pt1", [128, 528])
    PT2 = sb("pt2", [128, 528])
    POOLED = sb("pooled", [128, 4, 512], BF16)
    PTP = sb("ptp", [128, 4, 512], BF16)
    MG = sb("mg", [128, 2, 512])
    RL = sb("rl", [128, 2, 512])
    WR = sb("wr", [128, NW, 4096], BF16)
    RING = sb("ring", [128, NR, KVW], BF16)
    PTL = sb("ptl", [128, NP, 512], BF16)
    RRB = sb("rrb", [128, 2, 512], BF16)
    CT = sb("ct", [128, 64, H])
    BIAS = sb("bias", [128, 64, H])
    CSM = sb("csm", [128, 16, H])
    BSM = sb("bsm", [128, 16, H])
    LG = sb("lg", [128, 8, H])
    LGT = sb("lgt", [128, 8, H])
    CBT = sb("cbt", [128, 9, H])
    STT = sb("stt", [128, 4, 2, 6])
    MV = sb("mv", [128, 4, 8])
    POUT = sb("pout", [16, 512])
    SPT = POUT
    KM = sb("km", [128, 4, 16], BF16)
    VM = sb("vm", [16, 4, SLAB], BF16)
    VMS = sb("vms", [16, 2, SLAB], BF16)
    print("sbuf bytes remaining/partition:", nc.sbuf_bytes_remaining() if callable(nc.sbuf_bytes_remaining) else nc.sbuf_bytes_remaining)

    PSALL = nc.alloc_psum_tensor("psall", [128, 4096], F32)
    PS = [PSALL[:, 512 * i:512 * (i + 1)] for i in range(8)]
    PSB = [Buf(f"ps{i}") for i in range(8)]

    names = ["const", "cb16", "bfb", "psc", "lnp", "xtra", "xtrb", "qt", "kvst", "ub", "um", "us", "pt1", "pt2",
             "pooled", "ptp", "ct", "bias", "csm", "bsm", "lg", "lgt", "cbt", "pout", "km", "vm", "carry", "rrb", "vms0", "vms1"]
    B = {n: Buf(n) for n in names}
    XTb = [Buf(f"xt{t}") for t in range(4)]
    TMPNb = [Buf(f"tmpn{t}") for t in range(4)]
    XBNb = [Buf(f"xbn{t}") for t in range(4)]
    Rb = [Buf(f"r{t}") for t in range(4)]
    MGb = [Buf(f"mg{t}") for t in range(2)]
    RLb = [Buf(f"rl{t}") for t in range(2)]
    KOUT, KOUTb = MG, MGb
    VOUT, VOUTb = RL, RLb
    WRb = [Buf(f"wr{t}") for t in range(NW)]
    RINGb = [Buf(f"ring{t}") for t in range(NR)]
    PTLb = [Buf(f"ptl{t}") for t in range(NP)]
    STTb = [Buf(f"stt{t}") for t in range(4)]
    MVb = [Buf(f"mv{t}") for t in range(4)]
    WSb = [[Buf(f"ws{l}_{c}") for c in range(NCH)] for l in range(DEPTH)]
    KVSb = [[Buf(f"kvs{l}_{i}") for i in range(NBLK)] for l in range(DEPTH)]
    XSb = [Buf(f"xs{i}") for i in range(NBLK + 1)]

    WL = [Lane(nc, f"w{t}") for t in range(NW)]
    RLn = [Lane(nc, f"r{t}") for t in range(NR)]
    XL = [Lane(nc, f"x{t}") for t in range(4)]
    XBL = [Lane(nc, f"xb{t}") for t in range(4)]
    OL = [Lane(nc, f"o{t}") for t in range(8)]
    CLn = [Lane(nc, f"c{t}") for t in range(8)]
    ML = [Lane(nc, f"m{t}") for t in range(4)]
    rr_ctr = {"o": 0, "c": 0, "m": 0, "g": 0, "p": 0, "rl": 0, "mg": 0, "xb": 0, "ko": 0, "vo": 0, "yst": 0,
              "tmpn": 0, "ring": 0}

    def nxt(k, n):
        v = rr_ctr[k]
        rr_ctr[k] = (v + 1) % n
        return v

    def olane():
        return OL[nxt("o", 8)]

    def mlane():
        return ML[nxt("m", 4)]

    V = nc.vector
    A = nc.scalar
    G = nc.gpsimd
    T = nc.tensor

    dma(SP, mlane(), CONST[:], consts, writes=[B["const"]])
    dma(SP, mlane(), BFB[:], b_f.rearrange("l h -> (l h)").partition_broadcast(128), writes=[B["bfb"]])
    with nc.allow_non_contiguous_dma(reason="tiny one-time param load"):
        dma(SP, mlane(), PSC[:].rearrange("c (l g) -> c l g", l=DEPTH),
            pool_scale.rearrange("l (g c) -> c l g", g=4), writes=[B["psc"]])
    op(POOL, [lambda: G.tensor_copy(out=TRIB, in_=TRIF), lambda: G.tensor_copy(out=IDB, in_=IDF),
              lambda: G.tensor_copy(out=SELB, in_=SELF)],
       reads=[B["const"]], writes=[B["cb16"]])
    op(POOL, [lambda: G.memset(QT[:], 0.0)], writes=[B["qt"]])
    op(POOL, [lambda: G.memset(RRB[:], 0.0)], writes=[B["rrb"]])
    op(POOL, [lambda: G.memset(RL[:], 0.0)], writes=[RLb[0], RLb[1]])
    op(POOL, [lambda: G.memset(KVST[:], 1.0)], writes=[B["kvst"]])
    for s in range(NR):
        op(POOL, [lambda s=s: G.memset(RING[:, s, :], 1.0)], writes=[RINGb[s]])
    op(POOL, [lambda: G.memset(VM[:], 1.0)], writes=[B["vm"]])
    op(POOL, [lambda: G.memset(UM[:], 0.0)], writes=[B["um"]])
    op(POOL, [lambda: G.memset(US[:], 0.0)], writes=[B["us"]])
    op(POOL, [lambda: G.memset(UB[:], 0.0)], writes=[B["ub"]])

    def chunk_src(l, c):
        def kcv(ap):
            return ap.rearrange("(kc p) n -> p kc n", p=128)
        if c == 0:
            return kcv(w_in[l][:, OFF_F:OFF_F + 8])
        if c == 1:
            return kcv(w_in[l][:, OFF_K:OFF_K + 512])
        if c == 2:
            return kcv(w_in[l][:, OFF_V:OFF_V + 512])
        if c == 3:
            return kcv(w_in[l][:, OFF_Q:OFF_Q + 512])
        if c == 4:
            return kcv(w_in[l][:, OFF_U:OFF_U + 512])
        if c in (5, 6):
            o = OFF_GA + 512 * (c - 5)
            return kcv(w_in[l][:, o:o + 512])
        if c in (7, 8):
            o = OFF_GP + 512 * (c - 7)
            return kcv(w_in[l][:, o:o + 512])
        if c == 9:
            return w_pool_grp[l].rearrange("g c e -> c g e")
        if c in (10, 12):
            j = (c - 10) // 2
            return kcv(w_attn_br[l][:, 512 * j:512 * j + 512])
        if c in (11, 13):
            j = (c - 11) // 2
            return kcv(w_pool_br[l][:, 512 * j:512 * j + 512])
        if c in (14, 15):
            j = c - 14
            return kcv(w_out[l][:, 512 * j:512 * j + 512])
        if 16 <= c < 24:
            j = c - 16
            return kcv(w_up[l][:, 512 * j:512 * j + 512])
        j = c - 24
        half, g = j // 4, j % 4
        return kcv(w_down[l][1024 * g:1024 * g + 1024, 512 * half:512 * half + 512])

    def chunk_shape(c):
        if c == 0:
            return (128, 8, 8)
        if c == 9:
            return (128, 4, 128)
        if c in (10, 11, 12, 13):
            return (128, 4, 512)
        return (128, 8, 512)

    def ws_view(l, c):
        P_, A_, B_ = chunk_shape(c)
        return WS[l, c][0:P_, 0:A_ * B_].rearrange("p (a b) -> p a b", a=A_)

    def wr_view(slot, c):
        P_, A_, B_ = chunk_shape(c)
        return WR[0:P_, slot, 0:A_ * B_].rearrange("p (a b) -> p a b", a=A_)

    def emit_casts(l):
        for c in range(NCH):
            dma(POOL, CLn[nxt("c", 8)], ws_view(l, c), chunk_src(l, c), writes=[WSb[l][c]])

    wseq = []
    for l in range(NL):
        for _b in range(NB + 1):
            for c in range(NCH):
                wseq.append((l, c))
    wstate = {"next_load": 0, "next_use": 0}

    def w_prefetch(upto):
        while wstate["next_load"] < min(upto, len(wseq)):
            i = wstate["next_load"]
            l, c = wseq[i]
            slot = i % NW
            dma(SP, WL[slot], wr_view(slot, c), ws_view(l, c), reads=[WSb[l][c]], writes=[WRb[slot]])
            wstate["next_load"] += 1

    def w_get(l, c, hold=0):
        i = wstate["next_use"]
        assert wseq[i] == (l, c), (wseq[i], l, c)
        w_prefetch(i + NW - hold)
        wstate["next_use"] += 1
        slot = i % NW
        return wr_view(slot, c), WRb[slot]

    ring_seq = []
    for l_ in range(NL):
        for hp_ in range(4):
            for kb_ in range(2):
                ring_seq.append(("sample", l_, -1, hp_, kb_))
        for i_ in range(NB):
            for hp_ in range(4):
                for kb_ in range(i_):
                    ring_seq.append(("main", l_, i_, hp_, kb_))
    ring_state = {"next_load": 0, "next_use": 0}
    published = set()
    RING_LA = NR - 2

    def ring_get(kind, l, i, hp, kb):
        n = ring_state["next_use"]
        assert ring_seq[n] == (kind, l, i, hp, kb), (ring_seq[n], kind, l, i, hp, kb)
        while ring_state["next_load"] < len(ring_seq) and ring_state["next_load"] <= n + RING_LA:
            m = ring_state["next_load"]
            e = ring_seq[m]
            if m > n and (e[0] == "sample" or (e[1], e[4]) not in published):
                break
            if e[0] == "main":
                sl = m % NR
                dma(SP, RLn[sl], RING[:, sl, :], KVS[e[1], e[4]][:, e[3], :], reads=[KVSb[e[1]][e[4]]], writes=[RINGb[sl]])
            ring_state["next_load"] += 1
        ring_state["next_use"] += 1
        return n % NR

    def mm(out, lhsT, rhs, start, stop):
        return lambda: T.matmul(out, lhsT=lhsT, rhs=rhs, start=start, stop=stop)

    gen_banks = [0, 1, 2, 3]

    def gbank():
        return gen_banks[nxt("g", 4)]

    def transposes(tl_list, src_fn, dst, dstbuf):
        for tl in tl_list:
            col0, n = tl["col0"], tl["n"]
            src, sbuf_ = src_fn(tl)
            bk = gbank()
            pb = PS[bk][:].bitcast(BF16)
            fns = [(lambda kc=kc, n=n, src=src, pb=pb: T.transpose(out=pb[:, kc * 128:kc * 128 + n],
                                                                    in_=src[:, kc * 128:(kc + 1) * 128],
                                                                    identity=IDB[0:n, 0:n])) for kc in range(8)]
            op(PE, fns, reads=[sbuf_, B["cb16"]], writes=[PSB[bk]])
            op(DVE, [lambda n=n, col0=col0, pb=pb: V.tensor_copy(
                out=dst[:, :, col0:col0 + n],
                in_=pb.rearrange("p (k t) -> p k t", k=8)[:, :, 0:n])],
               reads=[PSB[bk]], writes=[dstbuf], disjoint=True)

    def layer_norm_multi(tls, gi, bi, in_place, after=None, kbase=0):
        K_ = len(tls)
        for k, tl in enumerate(tls, kbase):
            t, n = tl["t"], tl["n"]
            yield op(DVE, [lambda: V.bn_stats(out=STT[0:n, k, 0, :], in_=XT[0:n, t, 0:512]),
                           lambda: V.bn_stats(out=STT[0:n, k, 1, :], in_=XT[0:n, t, 512:1024])],
                     reads=[XTb[t]], writes=[STTb[k]])
            yield op(DVE, [lambda: V.bn_aggr(out=MV[0:n, k, 0:2], in_=STT[0:n, k].rearrange("p a b -> p (a b)"))],
                     reads=[STTb[k]], writes=[MVb[k]])
        for k, tl in enumerate(tls, kbase):
            t, n = tl["t"], tl["n"]
            yield op(ACT, [lambda: A.activation(out=MV[0:n, k, 2:3], in_=MV[0:n, k, 1:2], func=AF.Ln, bias=LN_EPS, scale=1.0)],
                     reads=[MVb[k]], writes=[MVb[k]])
            yield op(ACT, [lambda: A.activation(out=MV[0:n, k, 3:4], in_=MV[0:n, k, 2:3], func=AF.Exp, scale=-0.5)],
                     reads=[MVb[k]], writes=[MVb[k]])
        for k, tl in enumerate(tls, kbase):
            t, n = tl["t"], tl["n"]
            yield op(DVE, [lambda: V.scalar_tensor_tensor(out=MV[0:n, k, 4:5], in0=MV[0:n, k, 0:1], scalar=-1.0,
                                                          in1=MV[0:n, k, 3:4], op0=ALU.mult, op1=ALU.mult)],
                     reads=[MVb[k]], writes=[MVb[k]])
        for k, tl in enumerate(tls, kbase):
            t, n = tl["t"], tl["n"]
            yield op(ACT, [lambda: A.activation(out=TMPN[0:n, k, :], in_=XT[0:n, t, :], func=AF.Identity,
                                                scale=MV[0:n, k, 3:4], bias=MV[0:n, k, 4:5])],
                     reads=[MVb[k], XTb[t]], writes=[TMPNb[k]])
        for k, tl in enumerate(tls, kbase):
            t, n = tl["t"], tl["n"]
            yield op(DVE, [lambda: V.tensor_tensor(out=TMPN[0:n, k, :], in0=TMPN[0:n, k, :], in1=LNP[0:n, gi, :], op=ALU.mult)],
                     reads=[B["lnp"], TMPNb[k]], writes=[TMPNb[k]])
        for k, tl in enumerate(tls, kbase):
            t, n = tl["t"], tl["n"]
            if in_place:
                yield op(POOL, [lambda: G.tensor_tensor(out=XT[0:n, t, :], in0=TMPN[0:n, k, :], in1=LNP[0:n, bi, :], op=ALU.add)],
                         reads=[B["lnp"], TMPNb[k]], writes=[XTb[t]])
            else:
                yield op(POOL, [lambda: G.tensor_tensor(out=TMPN[0:n, k, :], in0=TMPN[0:n, k, :], in1=LNP[0:n, bi, :], op=ALU.add)],
                         reads=[B["lnp"], TMPNb[k]], writes=[TMPNb[k]])
                if after is not None:
                    yield after(k, tl)
        if in_place:
            for k, tl in enumerate(tls, kbase):
                t, n = tl["t"], tl["n"]
                yield op(ACT, [lambda: A.activation(out=XBN[0:n, t, :], in_=XT[0:n, t, :], func=AF.Copy)],
                         reads=[XTb[t]], writes=[XBNb[t]])

    def scale_slab(dst, src, nk, wA, wB, rbufs, wbuf):
        op(POOL, [lambda: G.tensor_scalar(out=dst[0:nk, 0:65], in0=src[0:nk, 0:65], scalar1=wA, scalar2=None, op0=ALU.mult),
                  lambda: G.tensor_scalar(out=dst[0:nk, 66:162], in0=src[0:nk, 66:162], scalar1=wB, scalar2=None, op0=ALU.mult)],
           reads=rbufs, writes=[wbuf])

    def attention(q0, nq, ktile_fn):
        ATT = R[:, 3, 0:2048].rearrange("p (h t) -> p h t", h=4)
        pending = [None]
        for hp in range(4):
            tiles = ktile_fn(hp)
            ob = [4, 5] if hp % 2 == 0 else [6, 7]
            nt = len(tiles)
            sb_of = {}

            def emit_S(idx):
                if callable(tiles[idx]):
                    tiles[idx] = tiles[idx]()
                kt = tiles[idx]
                nk, qlo = kt["nk"], kt["qlo"]
                pr = idx % 2
                for hh in range(2):
                    bk = 2 * pr + hh
                    op(PE, [mm(PS[bk][0:nk, qlo:nq], kt["K"], QT[:, hh, hp, q0 + qlo:q0 + nq], True, True)],
                       reads=[B["qt"]] + kt["bufs"], writes=[PSB[2 * pr], PSB[2 * pr + 1]])
                sb_of[idx] = pr

            emit_S(0)
            if nt > 1:
                emit_S(1)
            for idx in range(nt):
                kt = tiles[idx]
                nk, qlo = kt["nk"], kt["qlo"]
                pr = sb_of[idx]
                pp = nxt("p", 2)
                op(ACT, [lambda: A.activation(
                    out=PTL[0:nk, 2 * pp:2 * pp + 2, qlo:nq],
                    in_=PSALL[0:nk, 1024 * pr:1024 * pr + 1024].rearrange("k (h c) -> k h c", h=2)[:, :, qlo:nq],
                    func=AF.Exp)],
                   reads=[PSB[2 * pr], PSB[2 * pr + 1]], writes=[PTLb[2 * pp]])
                if kt["diag"]:
                    w = min(nk, nq - qlo)
                    op(POOL, [lambda: G.tensor_tensor(
                        out=PTL[0:nk, 2 * pp:2 * pp + 2, qlo:qlo + w], in0=PTL[0:nk, 2 * pp:2 * pp + 2, qlo:qlo + w],
                        in1=TRIB[0:nk, 0:w].unsqueeze(1).broadcast_to([nk, 2, w]), op=ALU.mult)],
                       reads=[PTLb[2 * pp], B["cb16"]], writes=[PTLb[2 * pp]])
                if pending[0] is not None and idx == min(5, nt - 1):
                    pending[0](2 * pr)
                    pending[0] = None
                if idx + 2 < nt:
                    emit_S(idx + 2)
                for hh in range(2):
                    Vap = kt["V"][0:nk, 0:128] if hh == 0 else kt["V"][0:nk, 34:162]
                    op(PE, [mm(PS[ob[hh]][0:128, qlo:nq], Vap, PTL[0:nk, 2 * pp + hh, qlo:nq], idx == 0, idx == nt - 1)],
                       reads=[PTLb[2 * pp]] + kt["bufs"], writes=[PSB[ob[hh]]])
            oA, oB = ob
            LNS = 20.72326583694641
            op(ACT, [lambda: A.activation(out=RL[64:65, 0, 0:nq], in_=PS[oA][64:65, 0:nq], func=AF.Ln, scale=1.0e9),
                     lambda: A.activation(out=RL[32:33, 0, 0:nq], in_=PS[oB][32:33, 0:nq], func=AF.Ln, scale=1.0e9)],
               reads=[PSB[oA], PSB[oB]], writes=[RLb[0]])
            op(ACT, [lambda: A.activation(out=RL[64:65, 0, 0:nq], in_=RL[64:65, 0, 0:nq], func=AF.Exp, scale=-1.0, bias=LNS),
                     lambda: A.activation(out=RL[32:33, 0, 0:nq], in_=RL[32:33, 0, 0:nq], func=AF.Exp, scale=-1.0, bias=LNS)],
               reads=[RLb[0]], writes=[RLb[0]])
            op(DVE, [lambda: V.tensor_copy(out=RRB[0:65, 0, 0:nq], in_=RL[0:65, 0, 0:nq])],
               reads=[RLb[0]], writes=[B["rrb"]])
            op(DVE, [lambda: V.tensor_tensor(out=RRB[0:65, 1, 0:nq], in0=RL[0:65, 0, 0:nq], in1=RRB[0:65, 0, 0:nq],
                                             op=ALU.subtract)],
               reads=[RLb[0], B["rrb"]], writes=[B["rrb"]])

            def part2(bc, hp=hp, oA=oA, oB=oB):
                op(PE, [mm(PS[bc][:, 0:nq], SELB, RRB[:, 0, 0:nq], True, False),
                        mm(PS[bc][:, 0:nq], SELB, RRB[:, 1, 0:nq], False, True)],
                   reads=[B["rrb"], B["cb16"]], writes=[PSB[bc]])
                op(ACT, [lambda: A.activation(out=MG[:, 0, 0:nq], in_=PS[bc][:, 0:nq], func=AF.Copy)],
                   reads=[PSB[bc]], writes=[MGb[0]])
                op(DVE, [lambda: V.tensor_tensor(out=ATT[0:64, hp, q0:q0 + nq], in0=PS[oA][0:64, 0:nq],
                                                 in1=MG[0:64, 0, 0:nq], op=ALU.mult)],
                   reads=[PSB[oA], MGb[0]], writes=[Rb[3]], disjoint=True)
                op(DVE, [lambda: V.tensor_tensor(out=ATT[64:128, hp, q0:q0 + nq], in0=PS[oB][64:128, 0:nq],
                                                 in1=MG[64:128, 0, 0:nq], op=ALU.mult)],
                   reads=[PSB[oB], MGb[0]], writes=[Rb[3]], disjoint=True)
            pending[0] = part2
        pending[0](gbank())

    def stage_xload(blk):
        l = blk["l"]
        Tn = blk["T"]
        tiles = blk["tiles"]
        for tl in tiles:
            t, n = tl["t"], tl["n"]
            dma(SP, XL[t], XT[0:n, t, :], tl["xsrc"], reads=tl["xsbuf"], writes=[XTb[t]])

    def stage_xbf(blk):
        l = blk["l"]
        Tn = blk["T"]
        tiles = blk["tiles"]
        for tl in tiles:
            t, n = tl["t"], tl["n"]
            dma(POOL, XBL[t], XBN[0:n, t, :], tl["xsrc"], reads=tl["xsbuf"], writes=[XBNb[t]])

    def stage_xT(blk):
        l = blk["l"]
        Tn = blk["T"]
        tiles = blk["tiles"]
        transposes(tiles, lambda tl: (XBN[0:tl["n"], tl["t"], :], XBNb[tl["t"]]), XTRA, B["xtra"])

    def stage_P1(blk):
        l = blk["l"]
        Tn = blk["T"]
        tiles = blk["tiles"]
        WF, WFb = w_get(l, 0)
        for tl in tiles:
            t, n, col0 = tl["t"], tl["n"], tl["col0"]
            yield op(PE, [mm(PS[3][0:n, 8 * t:8 * t + 8], XTRA[:, kc, col0:col0 + n], WF[:, kc, 0:8], kc == 0, kc == 7)
                    for kc in range(8)], reads=[B["xtra"], WFb], writes=[PSB[3]])
        ntl = len(tiles)
        nmax = max(tl["n"] for tl in tiles)
        yield op(DVE, [lambda: V.tensor_tensor(out=LGT[0:nmax, 0:ntl, :],
                                         in0=PS[3][0:nmax, 0:8 * ntl].rearrange("p (t h) -> p t h", h=H),
                                         in1=BFB[0:nmax, l * H:(l + 1) * H].unsqueeze(1).broadcast_to([nmax, ntl, H]),
                                         op=ALU.add)],
           reads=[PSB[3], B["bfb"]], writes=[B["lgt"]])
        yield op(DVE, [lambda: V.tensor_scalar(out=LGT[0:nmax, 0:ntl, :], in0=LGT[0:nmax, 0:ntl, :], scalar1=-1.0, scalar2=60.0,
                                         op0=ALU.mult, op1=ALU.min)],
           reads=[B["lgt"]], writes=[B["lgt"]])
        yield op(ACT, [lambda: A.activation(out=LGT[0:nmax, 0:ntl, :], in_=LGT[0:nmax, 0:ntl, :], func=AF.Exp)],
           reads=[B["lgt"]], writes=[B["lgt"]])
        yield op(ACT, [lambda: A.activation(out=LGT[0:nmax, 0:ntl, :], in_=LGT[0:nmax, 0:ntl, :], func=AF.Ln, bias=1.0, scale=1.0)],
           reads=[B["lgt"]], writes=[B["lgt"]])
        yield op(DVE, [lambda: V.tensor_scalar(out=LG[0:nmax, 0:ntl, :], in0=LGT[0:nmax, 0:ntl, :], scalar1=-1.0, scalar2=None,
                                         op0=ALU.mult)],
           reads=[B["lgt"]], writes=[B["lg"]])
        for tl in tiles:
            t, n = tl["t"], tl["n"]
            with nc.allow_non_contiguous_dma(reason="logf rows are 32B"):
                yield dma(SP, olane(), tl["lout"], LG[0:n, t, :], reads=[B["lg"]])

        WK, WKb = w_get(l, 1)
        for tl in tiles:
            t, n, col0 = tl["t"], tl["n"], tl["col0"]
            bk = gbank()
            yield op(PE, [mm(PS[bk][0:n, 0:512], XTRA[:, kc, col0:col0 + n], WK[:, kc, :], kc == 0, kc == 7) for kc in range(8)],
               reads=[B["xtra"], WKb], writes=[PSB[bk]])
            ko = nxt("ko", 2)
            yield op(ACT, [lambda bk=bk, ko=ko, n=n: A.activation(out=KOUT[0:n, ko, :], in_=PS[bk][0:n, 0:512], func=AF.Copy)],
               reads=[PSB[bk]], writes=[KOUTb[ko]])
            yield dma(SP, olane(), tl["kout"], KOUT[0:n, ko, :], reads=[KOUTb[ko]])
        for hp in range(4):
            bk = gbank()
            yield op(PE, [mm(PS[bk][:, 0:Tn], WK[:, kc, 128 * hp:128 * hp + 128], XTRA[:, kc, 0:Tn], kc == 0, kc == 7)
                    for kc in range(8)], reads=[B["xtra"], WKb], writes=[PSB[bk]])
            yield op(DVE, [lambda bk=bk, hp=hp: V.tensor_copy(out=KVST[:, hp, 0:Tn], in_=PS[bk][:, 0:Tn])],
               reads=[PSB[bk]], writes=[B["kvst"]], disjoint=(hp > 0))
        WV, WVb = w_get(l, 2)
        for tl in tiles:
            t, n, col0 = tl["t"], tl["n"], tl["col0"]
            bk = gbank()
            yield op(PE, [mm(PS[bk][0:n, 0:512], XTRA[:, kc, col0:col0 + n], WV[:, kc, :], kc == 0, kc == 7) for kc in range(8)],
               reads=[B["xtra"], WVb], writes=[PSB[bk]])
            vo = nxt("vo", 2)
            yield op(DVE, [lambda bk=bk, vo=vo, n=n: V.tensor_copy(out=VOUT[0:n, vo, :], in_=PS[bk][0:n, 0:512])],
               reads=[PSB[bk]], writes=[VOUTb[vo]])
            yield dma(SP, olane(), tl["vout"], VOUT[0:n, vo, :], reads=[VOUTb[vo]])
            vt = tl["vt"]
            yield op(POOL, [lambda n=n, vt=vt: G.memset(KVST[0:n, :, 512 + SLAB * vt + 64:512 + SLAB * vt + 98], 1.0)],
                     writes=[B["kvst"]], disjoint=True)
            for hh in range(2):
                base = 512 + SLAB * vt + 98 * hh
                yield op(POOL, [lambda vo=vo, n=n, base=base, hh=hh: G.tensor_copy(
                    out=KVST[0:n, :, base:base + 64],
                    in_=VOUT[0:n, vo, :].rearrange("p (a b c) -> p a b c", a=4, b=2)[:, :, hh, :])],
                   reads=[VOUTb[vo]], writes=[B["kvst"]], disjoint=True)
        WQ, WQb = w_get(l, 3)
        for hp in range(4):
            bk = gbank()
            yield op(PE, [mm(PS[bk][:, 0:Tn], WQ[:, kc, 128 * hp:128 * hp + 128], XTRA[:, kc, 0:Tn], kc == 0, kc == 7)
                    for kc in range(8)], reads=[B["xtra"], WQb], writes=[PSB[bk]])
            yield op(ACT, [lambda bk=bk, hp=hp: A.activation(out=QT[0:64, 0, hp, 0:Tn], in_=PS[bk][0:64, 0:Tn], func=AF.Copy, scale=0.125),
                     lambda bk=bk, hp=hp: A.activation(out=QT[64:128, 1, hp, 0:Tn], in_=PS[bk][64:128, 0:Tn], func=AF.Copy, scale=0.125)],
               reads=[PSB[bk]], writes=[B["qt"]], disjoint=(hp > 0))

        blk["cumsum"](l)

        blk["publish"](l)


    def stage_P2(blk):
        l = blk["l"]
        Tn = blk["T"]
        tiles = blk["tiles"]
        WU, WUb = w_get(l, 4)
        for g in range(4):
            bk = gbank()
            yield op(PE, [mm(PS[bk][:, 0:Tn], WU[:, kc, 128 * g:128 * g + 128], XTRA[:, kc, 0:Tn], kc == 0, kc == 7)
                    for kc in range(8)], reads=[B["xtra"], WUb], writes=[PSB[bk]])
            for sg in blk["segs"]:
                ub, ubuf, c0, n = sg["U"], sg["Ubuf"], sg["col0"], sg["n"]
                yield op(ACT, [lambda bk=bk, g=g, ub=ub, c0=c0, n=n: A.activation(out=ub[:, g, 16:16 + n], in_=PS[bk][:, c0:c0 + n],
                                                                                func=AF.Copy)],
                   reads=[PSB[bk]], writes=[ubuf], disjoint=(g > 0))
        for sg in blk["segs"]:
            ub, ubuf, c0, n = sg["U"], sg["Ubuf"], sg["col0"], sg["n"]
            W_ = 16 + n
            if sg.get("pre"):
                sg["pre"](l)
            for g in range(4):
                w = 2 << g
                src = ub[:, g, :]
                cur = src
                curb = ubuf
                tmps = [(PT1, B["pt1"]), (PT2, B["pt2"])]
                sh = 1
                lo = 0
                for s in range(g + 1):
                    dst, dstb = tmps[s % 2]
                    lo2 = lo + sh
                    yield op(POOL, [lambda dst=dst, cur=cur, lo2=lo2, sh=sh, W_=W_: G.tensor_tensor(
                        out=dst[:, lo2:W_], in0=cur[:, lo2:W_], in1=cur[:, lo2 - sh:W_ - sh], op=ALU.add)],
                       reads=[curb], writes=[dstb])
                    cur, curb = dst[:, :] if False else dst, dstb
                    lo = lo2
                    sh *= 2
                if sg["kind"] == "meta":
                    yield op(DVE, [lambda cur=cur, g=g, n=n: V.tensor_tensor(out=cur[:, 16:16 + n], in0=cur[:, 16:16 + n],
                                                                       in1=INVC[:, 16 * g:16 * g + 16], op=ALU.mult)],
                       reads=[curb, B["const"]], writes=[curb])
                    yield op(DVE, [lambda cur=cur, g=g, n=n, c0=c0, src=src: V.tensor_tensor(
                        out=POOLED[:, g, c0:c0 + n], in0=cur[:, 16:16 + n], in1=src[:, 16:16 + n], op=ALU.subtract)],
                       reads=[curb, ubuf], writes=[B["pooled"]], disjoint=True)
                else:
                    yield op(DVE, [lambda cur=cur, g=g, n=n, c0=c0, src=src, w=w: V.scalar_tensor_tensor(
                        out=POOLED[:, g, c0:c0 + n], in0=cur[:, 16:16 + n], scalar=1.0 / w, in1=src[:, 16:16 + n],
                        op0=ALU.mult, op1=ALU.subtract)],
                       reads=[curb, ubuf], writes=[B["pooled"]], disjoint=True)
            if sg.get("post"):
                sg["post"](l)
        SG = [R[:, 0, :].rearrange("p (k t) -> p k t", k=8), R[:, 1, :].rearrange("p (k t) -> p k t", k=8)]
        for gi in range(2):
            for j in range(2):
                Wg, Wgb = w_get(l, 5 + 2 * gi + j)
                for cc in range(4):
                    oc = 4 * j + cc
                    bk = gbank()
                    yield op(PE, [mm(PS[bk][:, 0:Tn], Wg[:, kc, 128 * cc:128 * cc + 128], XTRA[:, kc, 0:Tn], kc == 0, kc == 7)
                            for kc in range(8)], reads=[B["xtra"], Wgb], writes=[PSB[bk]])
                    yield op(ACT, [lambda bk=bk, gi=gi, oc=oc: A.activation(out=SG[gi][:, oc, 0:Tn], in_=PS[bk][:, 0:Tn],
                                                                     func=AF.Sigmoid)],
                       reads=[PSB[bk]], writes=[Rb[gi]], disjoint=(oc > 0))

        WG, WGb = w_get(l, 9)
        for g in range(4):
            bk = gbank()
            yield op(PE, [mm(PS[bk][:, 0:Tn], WG[:, g, :], POOLED[:, g, 0:Tn], True, True)],
               reads=[B["pooled"], WGb], writes=[PSB[bk]])
            yield op(DVE, [lambda bk=bk, g=g: V.tensor_scalar(out=PTP[:, g, 0:Tn], in0=PS[bk][:, 0:Tn],
                                                        scalar1=PSC[:, 4 * l + g:4 * l + g + 1], scalar2=None, op0=ALU.mult)],
               reads=[PSB[bk], B["psc"]], writes=[B["ptp"]], disjoint=(g > 0))


    def stage_ATT(blk):
        l = blk["l"]
        Tn = blk["T"]
        tiles = blk["tiles"]
        for sg in blk["segs"]:
            attention(sg["col0"], sg["n"], lambda hp, sg=sg: sg["ktiles"](l, hp))


    def stage_MERGE_MIX(blk):
        l = blk["l"]
        Tn = blk["T"]
        tiles = blk["tiles"]
        SG = [R[:, 0, :].rearrange("p (k t) -> p k t", k=8), R[:, 1, :].rearrange("p (k t) -> p k t", k=8)]
        ATT = R[:, 3, 0:2048].rearrange("p (h t) -> p h t", h=4)
        MT = R[:, 2, :].rearrange("p (k t) -> p k t", k=8)
        for oc in range(8):
            j, cc = oc // 4, (oc % 4) * 128
            if oc % 4 == 0:
                WA, WAb = w_get(l, 10 + 2 * j)
                WP, WPb_ = w_get(l, 11 + 2 * j, hold=1)
            ba = gbank()
            op(PE, [mm(PS[ba][:, 0:Tn], WA[:, kc, cc:cc + 128], ATT[:, kc, 0:Tn], kc == 0, kc == 3) for kc in range(4)],
               reads=[Rb[3], WAb], writes=[PSB[ba]])
            bp = gbank()
            op(PE, [mm(PS[bp][:, 0:Tn], WP[:, kc, cc:cc + 128], PTP[:, kc, 0:Tn], kc == 0, kc == 3) for kc in range(4)],
               reads=[B["ptp"], WPb_], writes=[PSB[bp]])
            m0, m1 = 0, 1
            op(DVE, [lambda ba=ba, oc=oc: V.tensor_tensor(out=MG[:, 0, 0:Tn], in0=PS[ba][:, 0:Tn], in1=SG[0][:, oc, 0:Tn],
                                                          op=ALU.mult)],
               reads=[PSB[ba], Rb[0]], writes=[MGb[0]])
            op(DVE, [lambda bp=bp, oc=oc: V.tensor_tensor(out=MG[:, 1, 0:Tn], in0=PS[bp][:, 0:Tn], in1=SG[1][:, oc, 0:Tn],
                                                          op=ALU.mult)],
               reads=[PSB[bp], Rb[1]], writes=[MGb[1]])
            op(POOL, [lambda oc=oc: G.tensor_tensor(out=MT[:, oc, 0:Tn], in0=MG[:, 0, 0:Tn], in1=MG[:, 1, 0:Tn], op=ALU.add)],
               reads=[MGb[0], MGb[1]], writes=[Rb[2]], disjoint=(oc > 0))

        for j in range(2):
            WO, WOb = w_get(l, 14 + j)
            for tl in tiles:
                t, n, col0 = tl["t"], tl["n"], tl["col0"]
                bk = gbank()
                op(PE, [mm(PS[bk][0:n, 0:512], MT[:, kc, col0:col0 + n], WO[:, kc, :], kc == 0, kc == 7) for kc in range(8)],
                   reads=[Rb[2], WOb], writes=[PSB[bk]])
                op(DVE, [lambda bk=bk, t=t, n=n, j=j: V.scalar_tensor_tensor(
                    out=XT[0:n, t, 512 * j:512 * j + 512], in0=XT[0:n, t, 512 * j:512 * j + 512], scalar=ALPHA,
                    in1=PS[bk][0:n, 0:512], op0=ALU.mult, op1=ALU.add)],
                   reads=[PSB[bk], XTb[t]], writes=[XTb[t]])

    LN_TILE_MAJOR = bool(int(os.environ.get("MK_LNTM", "0")))

    def stage_LN1a(blk):
        if LN_TILE_MAJOR:
            for tl in blk["tiles"]:
                yield from layer_norm_multi([tl], 0, 1, True, None, tl["t"])
        else:
            yield from layer_norm_multi(blk["tiles"], 0, 1, True)

    def stage_LN1b(blk):
        transposes(blk["tiles"], lambda tl: (XBN[0:tl["n"], tl["t"], :], XBNb[tl["t"]]), XTRB, B["xtrb"])

    def stage_MLP(blk):
        l = blk["l"]
        Tn = blk["T"]
        tiles = blk["tiles"]
        for j in range(8):
            WUp, WUpb = w_get(l, 16 + j)
            for cc in range(4):
                fc = 4 * j + cc
                bk = gbank()
                op(PE, [mm(PS[bk][:, 0:Tn], WUp[:, kc, 128 * cc:128 * cc + 128], XTRB[:, kc, 0:Tn], kc == 0, kc == 7)
                        for kc in range(8)], reads=[B["xtrb"], WUpb], writes=[PSB[bk]])
                rl = nxt("mg", 2)
                op(ACT, [lambda bk=bk, rl=rl: A.activation(out=RL[:, rl, 0:Tn], in_=PS[bk][:, 0:Tn], func=AF.Relu)],
                   reads=[PSB[bk]], writes=[RLb[rl]])
                g, k = fc // 8, fc % 8
                op(POOL, [lambda rl=rl, g=g, k=k: G.tensor_tensor(out=R[:, g, 512 * k:512 * k + Tn], in0=RL[:, rl, 0:Tn],
                                                                  in1=RL[:, rl, 0:Tn], op=ALU.mult)],
                   reads=[RLb[rl]], writes=[Rb[g]], disjoint=(k > 0))
        for half in range(2):
            for g in range(4):
                WD, WDb = w_get(l, 24 + 4 * half + g)
                for tl in tiles:
                    t, n, col0 = tl["t"], tl["n"], tl["col0"]
                    op(PE, [mm(PS[4 + t][0:n, 0:512], R[:, g, 512 * k + col0:512 * k + col0 + n], WD[:, k, :],
                               g == 0 and k == 0, g == 3 and k == 7) for k in range(8)],
                       reads=[Rb[g], WDb], writes=[PSB[4 + t]])
            for tl in tiles:
                t, n = tl["t"], tl["n"]
                op(DVE, [lambda t=t, n=n, half=half: V.scalar_tensor_tensor(
                    out=XT[0:n, t, 512 * half:512 * half + 512], in0=XT[0:n, t, 512 * half:512 * half + 512], scalar=ALPHA,
                    in1=PS[4 + t][0:n, 0:512], op0=ALU.mult, op1=ALU.add)],
                   reads=[PSB[4 + t], XTb[t]], writes=[XTb[t]])

    def stage_LN2(blk):
        l = blk["l"]
        Tn = blk["T"]
        tiles = blk["tiles"]
        last_layer = (l == NL - 1)
        def after(k, tl):
            dst = tl["yout"][1 if last_layer else 0]
            if dst is None:
                return None
            return dma(SP, olane(), dst, TMPN[0:tl["n"], k, :], reads=[TMPNb[k]],
                       writes=tl["ybuf"] if not last_layer else [])
        if LN_TILE_MAJOR:
            for tl in tiles:
                yield from layer_norm_multi([tl], 2, 3, False, after, tl["t"])
        else:
            yield from layer_norm_multi(tiles, 2, 3, False, after)

    def main_block(i):
        tiles = []
        for t in range(4):
            r0 = 512 * i + 128 * t
            tiles.append(dict(t=t, col0=128 * t, n=128, vt=t,
                              xsrc=None, xsbuf=[],
                              kout=None, vout=None, lout=None,
                              yout=(XS[r0:r0 + 128, :], y_prompt[r0:r0 + 128, :]), ybuf=[XSb[i]], r0=r0))

        def cumsum(l):
            op(PE, [mm(PS[3][:, 64:96], TRIF, LG[:, 0:4, :].rearrange("p t h -> p (t h)"), True, True),
                    mm(PS[3][:, 96:128], ONESF, LG[:, 0:4, :].rearrange("p t h -> p (t h)"), True, True)],
               reads=[B["lg"], B["const"]], writes=[PSB[3]])
            tot = PS[3][:, 96:128].rearrange("p (t h) -> p t h", h=H)
            loc = PS[3][:, 64:96].rearrange("p (t h) -> p t h", h=H)
            op(DVE, [lambda: V.tensor_copy(out=CBT[:, 0, :], in_=CSM[:, 10, :])], reads=[B["csm"]], writes=[B["cbt"]])
            for t in range(1, 5):
                op(DVE, [lambda t=t: V.tensor_tensor(out=CBT[:, t, :], in0=CBT[:, t - 1, :], in1=tot[:, t - 1, :], op=ALU.add)],
                   reads=[PSB[3], B["cbt"]], writes=[B["cbt"]])
            op(DVE, [lambda: V.tensor_tensor(out=CT[:, 4 * i:4 * i + 4, :], in0=loc, in1=CBT[:, 0:4, :], op=ALU.add)],
               reads=[PSB[3], B["cbt"]], writes=[B["ct"]])
            op(DVE, [lambda: V.tensor_copy(out=CSM[:, 10, :], in_=CBT[:, 4, :])], reads=[B["cbt"]], writes=[B["csm"]])
            nt_ = 4 * i + 4
            op(DVE, [lambda: V.tensor_tensor(out=BIAS[:, 0:nt_, :], in0=CBT[:, 4:5, :].broadcast_to([128, nt_, H]),
                                             in1=CT[:, 0:nt_, :], op=ALU.subtract)],
               reads=[B["cbt"], B["ct"]], writes=[B["bias"]])
            op(DVE, [lambda: V.tensor_tensor(out=BSM[:, 0, :], in0=CBT[:, 4, :], in1=CSM[:, 0, :], op=ALU.subtract)],
               reads=[B["cbt"], B["csm"]], writes=[B["bsm"]])
            op(ACT, [lambda: A.activation(out=BIAS[:, 0:nt_, :], in_=BIAS[:, 0:nt_, :], func=AF.Exp)],
               reads=[B["bias"]], writes=[B["bias"]])
            op(ACT, [lambda: A.activation(out=BSM[0:16, 0, :], in_=BSM[0:16, 0, :], func=AF.Exp)],
               reads=[B["bsm"]], writes=[B["bsm"]])

        def publish(l):
            dma(SP, mlane(), KVS[l, i], KVST[:], reads=[B["kvst"]], writes=[KVSb[l][i]])
            published.add((l, i))

        def ktiles(l, hp):
            res = []
            hA, hB = 2 * hp, 2 * hp + 1
            vp = hp % 2
            vbuf = B["vms%d" % vp]
            scale_slab(VMS[:, vp, :], VM[:, hp, :], 16, BSM[0:16, 0, hA:hA + 1], BSM[0:16, 0, hB:hB + 1],
                       [B["vm"], B["bsm"]], vbuf)
            res.append(dict(K=KM[:, hp, :], V=VMS[:, vp, :], nk=16, qlo=0, diag=False, bufs=[B["km"], vbuf]))
            for kb in range(i):
                shared = {}
                for tt in range(4):
                    def mk(tt=tt, kb=kb, shared=shared):
                        if "s" not in shared:
                            s_ = ring_get("main", l, i, hp, kb)
                            shared["s"] = s_
                            for t2 in range(4):
                                g2 = 4 * kb + t2
                                vb2 = 512 + SLAB * t2
                                scale_slab(RING[:, s_, vb2:vb2 + SLAB], RING[:, s_, vb2:vb2 + SLAB], 128,
                                           BIAS[:, g2, hA:hA + 1], BIAS[:, g2, hB:hB + 1],
                                           [RINGb[s_], B["bias"]], RINGb[s_])
                        s_ = shared["s"]
                        vb = 512 + SLAB * tt
                        return dict(K=RING[:, s_, 128 * tt:128 * tt + 128], V=RING[:, s_, vb:vb + SLAB],
                                    nk=128, qlo=0, diag=False, bufs=[RINGb[s_]])
                    res.append(mk)
            for tt in range(4):
                g = 4 * i + tt
                vb = 512 + SLAB * tt
                scale_slab(KVST[:, hp, vb:vb + SLAB], KVST[:, hp, vb:vb + SLAB], 128,
                           BIAS[:, g, hA:hA + 1], BIAS[:, g, hB:hB + 1], [B["kvst"], B["bias"]], B["kvst"])
                res.append(dict(K=KVST[:, hp, 128 * tt:128 * tt + 128], V=KVST[:, hp, vb:vb + SLAB],
                                nk=128, qlo=128 * tt, diag=True, bufs=[B["kvst"]]))
            return res

        def post(l):
            op(POOL, [lambda: G.tensor_copy(out=UB[:, :, 0:16], in_=UB[:, :, 512:528])], reads=[B["ub"]], writes=[B["ub"]])

        def post_last(l):
            pool_out(l, UB, B["ub"], 513, pool_prompt)
            post(l)

        seg = dict(kind="main", col0=0, n=512, U=UB, Ubuf=B["ub"], ktiles=ktiles,
                   post=post_last if i == NBLK - 1 else post)
        return dict(T=512, tiles=tiles, segs=[seg], cumsum=cumsum, publish=publish, i=i)

    def pool_out(l, ub, ubuf, c_lo, dst):
        bk = gbank()
        op(PE, [(lambda g=g: T.transpose(out=PS[bk][0:15, 128 * g:128 * g + 128], in_=ub[:, g, c_lo:c_lo + 15],
                                         identity=IDF)) for g in range(4)],
           reads=[ubuf, B["const"]], writes=[PSB[bk]])
        op(DVE, [lambda: V.tensor_copy(out=POUT[0:15, :], in_=PS[bk][0:15, 0:512])], reads=[PSB[bk]], writes=[B["pout"]])
        dma(SP, olane(), dst[l], POUT[0:15, :], reads=[B["pout"]])

    def mini_block():
        tiles = [dict(t=0, col0=0, n=16, vt=0, xsrc=None, xsbuf=[], kout=None, vout=None, lout=None,
                      yout=(XS[SEQ:SEQ + 16, :], None), ybuf=[XSb[NBLK]]),
                 dict(t=1, col0=16, n=32, vt=1, xsrc=None, xsbuf=[], kout=None, vout=None, lout=None,
                      yout=(XS[SEQ + 16:SEQ + 48, :], y_sample[:, :]), ybuf=[XSb[NBLK]])]

        def cumsum(l):
            op(PE, [mm(PS[3][0:16, 64:72], TRIF[0:16, 0:16], LG[0:16, 0, :], True, True),
                    mm(PS[3][:, 72:80], ONESF[0:16, :], LG[0:16, 0, :], True, True)],
               reads=[B["lg"], B["const"]], writes=[PSB[3]])
            op(DVE, [lambda: V.tensor_copy(out=CSM[0:16, 0, :], in_=PS[3][0:16, 64:72])], reads=[PSB[3]], writes=[B["csm"]])
            op(DVE, [lambda: V.tensor_copy(out=CSM[:, 10, :], in_=PS[3][:, 72:80])], reads=[PSB[3]], writes=[B["csm"]])
            op(DVE, [lambda: V.tensor_tensor(out=BSM[0:16, 3, :], in0=CSM[0:16, 10, :], in1=CSM[0:16, 0, :], op=ALU.subtract)],
               reads=[B["csm"]], writes=[B["bsm"]])
            with nc.allow_non_contiguous_dma(reason="cache logf rows are 32B"):
                dma(SP, mlane(), LGT[:, 0:8, :], cache_logf[l].rearrange("(t p) h -> p t h", p=128), writes=[B["lgt"]])
            op(PE, [mm(PS[3][:, 128:192], TRIF, LGT[:, 0:8, :].rearrange("p t h -> p (t h)"), True, True),
                    mm(PS[3][:, 192:256], ONESF, LGT[:, 0:8, :].rearrange("p t h -> p (t h)"), True, True),
                    mm(PS[3][0:32, 256:264], TRIF[0:32, 0:32], LG[0:32, 1, :], True, True),
                    mm(PS[3][:, 264:272], ONESF[0:32, :], LG[0:32, 1, :], True, True)],
               reads=[B["lg"], B["lgt"], B["const"]], writes=[PSB[3]])
            tot = PS[3][:, 192:256].rearrange("p (t h) -> p t h", h=H)
            loc = PS[3][:, 128:192].rearrange("p (t h) -> p t h", h=H)
            op(DVE, [lambda: V.memset(CBT[:, 0, :], 0.0)], writes=[B["cbt"]])
            for t in range(1, 9):
                op(DVE, [lambda t=t: V.tensor_tensor(out=CBT[:, t, :], in0=CBT[:, t - 1, :], in1=tot[:, t - 1, :], op=ALU.add)],
                   reads=[PSB[3], B["cbt"]], writes=[B["cbt"]])
            op(DVE, [lambda: V.tensor_tensor(out=CSM[:, 2:10, :], in0=loc, in1=CBT[:, 0:8, :], op=ALU.add)],
               reads=[PSB[3], B["cbt"]], writes=[B["csm"]])
            op(DVE, [lambda: V.tensor_tensor(out=CSM[0:32, 1, :], in0=PS[3][0:32, 256:264], in1=CBT[0:32, 8, :], op=ALU.add)],
               reads=[PSB[3], B["cbt"]], writes=[B["csm"]])
            op(DVE, [lambda: V.tensor_tensor(out=CSM[:, 11, :], in0=PS[3][:, 264:272], in1=CBT[:, 8, :], op=ALU.add)],
               reads=[PSB[3], B["cbt"]], writes=[B["csm"]])
            op(DVE, [lambda: V.tensor_tensor(out=BSM[:, 4:12, :], in0=CSM[:, 11:12, :].broadcast_to([128, 8, H]),
                                             in1=CSM[:, 2:10, :], op=ALU.subtract)],
               reads=[B["csm"]], writes=[B["bsm"]])
            op(DVE, [lambda: V.tensor_tensor(out=BSM[0:32, 1, :], in0=CSM[0:32, 11, :], in1=CSM[0:32, 1, :], op=ALU.subtract)],
               reads=[B["csm"]], writes=[B["bsm"]])
            op(ACT, [lambda: A.activation(out=BSM[0:16, 3, :], in_=BSM[0:16, 3, :], func=AF.Exp),
                     lambda: A.activation(out=BSM[:, 4:12, :], in_=BSM[:, 4:12, :], func=AF.Exp),
                     lambda: A.activation(out=BSM[0:32, 1, :], in_=BSM[0:32, 1, :], func=AF.Exp)],
               reads=[B["bsm"]], writes=[B["bsm"]])

        def publish(l):
            op(POOL, [lambda: G.tensor_copy(out=KM[:, :, :], in_=KVST[:, :, 0:16])], reads=[B["kvst"]], writes=[B["km"]])
            op(POOL, [lambda: G.tensor_copy(out=VM[0:16, :, :], in_=KVST[0:16, :, 512:512 + SLAB])],
               reads=[B["kvst"]], writes=[B["vm"]])

        def ktiles_meta(l, hp):
            hA, hB = 2 * hp, 2 * hp + 1
            vp = hp % 2
            vbuf = B["vms%d" % vp]
            scale_slab(VMS[:, vp, :], VM[:, hp, :], 16, BSM[0:16, 3, hA:hA + 1], BSM[0:16, 3, hB:hB + 1],
                       [B["vm"], B["bsm"]], vbuf)
            return [dict(K=KM[:, hp, :], V=VMS[:, vp, :], nk=16, qlo=0, diag=True, bufs=[B["km"], vbuf])]

        def ktiles_sample(l, hp):
            res = []
            CKB = R[:, 2, :].rearrange("p (t c) -> p t c", t=8)
            for kb in range(2):
                s = ring_get("sample", l, -1, hp, kb)
                bk = gbank()
                pb = PS[bk][:].bitcast(BF16)
                op(PE, [(lambda tt=tt, kb=kb, pb=pb: T.transpose(out=pb[:, 128 * tt:128 * tt + 128],
                                                               in_=CKB[:, 4 * kb + tt, 128 * hp:128 * hp + 128],
                                                               identity=IDB)) for tt in range(4)],
                   reads=[Rb[2], B["cb16"]], writes=[PSB[bk]])
                op(DVE, [lambda s=s, pb=pb: V.tensor_copy(out=RING[:, s, 0:512], in_=pb[:, 0:512])],
                   reads=[PSB[bk]], writes=[RINGb[s]])
                for tt in range(4):
                    op(POOL, [lambda tt=tt: G.memset(RING[:, s, 512 + SLAB * tt + 64:512 + SLAB * tt + 98], 1.0)],
                       writes=[RINGb[s]], disjoint=True)
                for tt in range(4):
                    r0 = 512 * kb + 128 * tt
                    src = cache_v[l][r0:r0 + 128, 128 * hp:128 * hp + 128].rearrange("p (b c) -> p b c", b=2)
                    for hh in range(2):
                        base = 512 + SLAB * tt + 98 * hh
                        dma(POOL, CLn[nxt("c", 8)], RING[:, s, base:base + 64], src[:, hh, :], writes=[RINGb[s]], disjoint=True)
                for tt in range(4):
                    g = 4 * kb + tt
                    vb = 512 + SLAB * tt
                    scale_slab(RING[:, s, vb:vb + SLAB], RING[:, s, vb:vb + SLAB], 128,
                               BSM[:, 4 + g, 2 * hp:2 * hp + 1], BSM[:, 4 + g, 2 * hp + 1:2 * hp + 2],
                               [RINGb[s], B["bsm"]], RINGb[s])
                    res.append(dict(K=RING[:, s, 128 * tt:128 * tt + 128], V=RING[:, s, vb:vb + SLAB],
                                    nk=128, qlo=0, diag=False, bufs=[RINGb[s]]))
            vb = 512 + SLAB
            scale_slab(KVST[:, hp, vb:vb + SLAB], KVST[:, hp, vb:vb + SLAB], 32,
                       BSM[0:32, 1, 2 * hp:2 * hp + 1], BSM[0:32, 1, 2 * hp + 1:2 * hp + 2], [B["kvst"], B["bsm"]], B["kvst"])
            res.append(dict(K=KVST[:, hp, 16:48], V=KVST[:, hp, vb:vb + SLAB], nk=32, qlo=0, diag=True, bufs=[B["kvst"]]))
            return res

        def pre_sample(l):
            dma(SP, mlane(), SPT[0:15, :], state_pool[l], writes=[B["pout"]])
            bk = gbank()
            op(PE, [(lambda g=g: T.transpose(out=PS[bk][:, 16 * g:16 * g + 15], in_=SPT[0:15, 128 * g:128 * g + 128],
                                             identity=IDF[0:15, 0:15])) for g in range(4)],
               reads=[B["pout"], B["const"]], writes=[PSB[bk]])
            op(DVE, [lambda: V.tensor_copy(out=US[:, :, 1:16], in_=PS[bk][:, 0:64].rearrange("p (g c) -> p g c", g=4)[:, :, 0:15])],
               reads=[PSB[bk]], writes=[B["us"]])

        def post_sample(l):
            pool_out(l, US, B["us"], 33, pool_sample)

        def post_meta(l):
            op(POOL, [lambda: G.tensor_copy(out=UB[:, :, 0:16], in_=UM[:, :, 16:32])], reads=[B["um"]], writes=[B["ub"]])

        segs = [dict(kind="meta", col0=0, n=16, U=UM, Ubuf=B["um"], ktiles=ktiles_meta, post=post_meta),
                dict(kind="sample", col0=16, n=32, U=US, Ubuf=B["us"], ktiles=ktiles_sample, pre=pre_sample, post=post_sample)]
        return dict(T=48, tiles=tiles, segs=segs, cumsum=cumsum, publish=publish, i=-1)

    blocks = []
    for l in range(NL):
        mb = mini_block()
        mb["l"] = l
        mb["first"] = True
        mt, st = mb["tiles"]
        if l == 0:
            mt["xsrc"], st["xsrc"] = meta_tokens, x_sample
        else:
            mt["xsrc"], st["xsrc"] = XS[SEQ:SEQ + 16, :], XS[SEQ + 16:SEQ + 48, :]
            mt["xsbuf"] = st["xsbuf"] = [XSb[NBLK]]
        mt["kout"], mt["vout"], mt["lout"] = k_prompt[l, 0:16, :], v_prompt[l, 0:16, :], logf_prompt[l, 0:16, :]
        st["kout"], st["vout"], st["lout"] = k_sample[l], v_sample[l], logf_sample[l]
        blocks.append(mb)
        for i in range(NB):
            blk = main_block(i)
            blk["l"] = l
            blk["first"] = False
            for tl in blk["tiles"]:
                r0 = tl["r0"]
                if l == 0:
                    tl["xsrc"] = x_prompt[r0:r0 + 128, :]
                else:
                    tl["xsrc"] = XS[r0:r0 + 128, :]
                    tl["xsbuf"] = [XSb[i]]
                tl["kout"] = k_prompt[l, 16 + r0:16 + r0 + 128, :]
                tl["vout"] = v_prompt[l, 16 + r0:16 + r0 + 128, :]
                tl["lout"] = logf_prompt[l, 16 + r0:16 + r0 + 128, :]
            blocks.append(blk)

    P1C, P2C, MMC, MLPC = [0, 1, 2, 3], [4, 5, 6, 7, 8, 9], [10, 11, 12, 13, 14, 15], list(range(16, 32))
    wseq.clear()
    wseq.extend((blocks[0]["l"], c) for c in P1C)
    for k, b in enumerate(blocks):
        nb_ = blocks[k + 1] if k + 1 < len(blocks) else None
        wseq.extend((b["l"], c) for c in P2C + MMC)
        if nb_ is not None:
            wseq.extend((nb_["l"], c) for c in P1C)
        wseq.extend((b["l"], c) for c in MLPC)

    def drive(*gens):
        gens = [g for g in gens if g is not None]
        while gens:
            for g in list(gens):
                try:
                    next(g)
                except StopIteration:
                    gens.remove(g)

    stage_xbf(blocks[0])
    emit_casts(0)
    stage_xT(blocks[0])
    drive(stage_P1(blocks[0]))
    if len(blocks) > 1:
        stage_xbf(blocks[1])
    prev = None
    for k, b in enumerate(blocks):
        nb_ = blocks[k + 1] if k + 1 < len(blocks) else None
        nnb_ = blocks[k + 2] if k + 2 < len(blocks) else None
        l = b["l"]
        drive(stage_LN2(prev) if prev is not None else None, stage_P2(b))
        if b["first"]:
            for kk, src in enumerate([ln1_g, ln1_b, ln2_g, ln2_b]):
                dma(SP, mlane(), LNP[:, kk, :], src[l].partition_broadcast(128), writes=[B["lnp"]], disjoint=(kk > 0))
            dma(POOL, CLn[nxt("c", 8)], R[:, 2, :].rearrange("p (t c) -> p t c", t=8),
                cache_k[l].rearrange("(t p) c -> p t c", p=128), writes=[Rb[2]])
        stage_xload(b)
        if nb_ is not None:
            stage_xT(nb_)
        if k == 0 and NL > 1:
            emit_casts(1)
        stage_ATT(b)
        stage_MERGE_MIX(b)
        drive(stage_LN1a(b), stage_P1(nb_) if nb_ is not None else None)
        stage_LN1b(b)
        if nnb_ is not None:
            stage_xbf(nnb_)
        stage_MLP(b)
        prev = b
    drive(stage_LN2(prev))

    for ln in OL + ML + XL + XBL + WL + RLn + CLn:
        SP.wait(ln.last)
    for e in (PE, ACT, DVE, POOL):
        if e.count:
            SP.eng.wait_ge(e.sem, e.count)
    print("instr counts:", {e.name: e.count for e in (PE, ACT, DVE, POOL)}, "waits:",
          {e.name: e.nwaits for e in (PE, ACT, DVE, POOL, SP)})
    return nc


def make_consts():
    c = np.zeros((128, 576), np.float32)
    c[64, 448:512] = 1.0
    c[32, 512:576] = 1.0
    s = np.arange(128)
    c[:, 0:128] = (s[:, None] <= s[None, :]).astype(np.float32)
    c[:, 128:256] = 1.0
    c[:, 256:384] = np.eye(128, dtype=np.float32)
    for g, w in enumerate((2, 4, 8, 16)):
        for p in range(16):
            c[:, 384 + 16 * g + p] = 1.0 / min(w, p + 1)
    return c


_NC_CACHE = {}


def kernel(x_prompt, x_sample, cache_k, cache_v, cache_logf, state_pool, meta_tokens,
           w_in, b_f, w_pool_grp, pool_scale, w_attn_br, w_pool_br, w_out,
           ln1_g, ln1_b, w_up, w_down, ln2_g, ln2_b):
    NB = int(os.environ.get("MK_NB", NBLK))
    NL = int(os.environ.get("MK_NL", DEPTH))
    f = lambda a: np.ascontiguousarray(np.asarray(a, dtype=np.float32))
    nc = build(NB, NL)
    consts = make_consts()
    shared = dict(meta_tokens=f(meta_tokens), w_in=f(w_in), b_f=f(b_f), w_pool_grp=f(w_pool_grp),
                  pool_scale=f(pool_scale), w_attn_br=f(w_attn_br), w_pool_br=f(w_pool_br), w_out=f(w_out),
                  ln1_g=f(ln1_g), ln1_b=f(ln1_b), w_up=f(w_up), w_down=f(w_down), ln2_g=f(ln2_g), ln2_b=f(ln2_b),
                  consts=consts)
    in_maps = []
    for b in range(8):
        m = dict(shared)
        m["x_prompt"] = f(x_prompt[b])
        m["x_sample"] = f(x_sample[b])
        m["cache_k"] = f(np.asarray(cache_k)[:, b].reshape(DEPTH, PAST, DA))
        m["cache_v"] = f(np.asarray(cache_v)[:, b].reshape(DEPTH, PAST, DA))
        m["cache_logf"] = f(np.asarray(cache_logf)[:, b])
        m["state_pool"] = f(np.asarray(state_pool)[:, b])
        in_maps.append(m)
    res = run_bass_kernel_spmd(nc, in_maps, core_ids=list(range(8)))
    rs = res.results
    st = lambda name, ax: np.stack([np.asarray(r[name]) for r in rs], axis=ax)
    y_prompt = st("y_prompt", 0)
    y_sample = st("y_sample", 0)
    k_prompt = st("k_prompt", 1).reshape(DEPTH, 8, L_TOT, H, DH)
    v_prompt = st("v_prompt", 1).reshape(DEPTH, 8, L_TOT, H, DH)
    logf_prompt = st("logf_prompt", 1)
    pool_prompt = st("pool_prompt", 1)
    k_sample = st("k_sample", 1).reshape(DEPTH, 8, TS, H, DH)
    v_sample = st("v_sample", 1).reshape(DEPTH, 8, TS, H, DH)
    logf_sample = st("logf_sample", 1)
    pool_sample = st("pool_sample", 1)
    return (y_prompt, y_sample, k_prompt, v_prompt, logf_prompt, pool_prompt,
            k_sample, v_sample, logf_sample, pool_sample)
```

```python
import os
import numpy as np
import concourse.bass as bass
import concourse.mybir as mybir
from concourse.bass_utils import run_bass_kernel_spmd

F32 = mybir.dt.float32
BF16 = mybir.dt.bfloat16
AF = mybir.ActivationFunctionType
ALU = mybir.AluOpType

D = 1024
SEQ = 8192
NMETA = 16
TS = 32
PAST = 1024
H = 8
DH = 64
DA = 512
DP = 512
DFF = 4096
DIN = 4104
OFF_Q, OFF_K, OFF_V, OFF_F, OFF_U, OFF_GA, OFF_GP = 0, 512, 1024, 1536, 1544, 2056, 3080
DEPTH = 2
ALPHA = (2.0 * DEPTH) ** 0.25
LN_EPS = 1e-5
L_TOT = NMETA + SEQ
NBLK = SEQ // 512
NCH = 32
SLAB = 162
KVW = 512 + 4 * SLAB
NW = 3
NR = 5
NP = 6


class Tok:
    __slots__ = ("sem", "key", "val", "clock")

    def __init__(self, sem, key, val, clock):
        self.sem, self.key, self.val, self.clock = sem, key, val, clock


class Eng:
    def __init__(self, nc, name, eng, same_wait):
        self.name = self.key = name
        self.eng = eng
        self.sem = nc.alloc_semaphore(name="e_" + name)
        self.count = 0
        self.seen = {}
        self.same_wait = same_wait
        self.nwaits = 0

    def wait(self, tok):
        if tok is None or self.seen.get(tok.key, 0) >= tok.val:
            return
        if not (tok.key == self.key and not self.same_wait):
            self.eng.wait_ge(tok.sem, tok.val)
            self.nwaits += 1
        self.seen[tok.key] = tok.val
        for k, v in tok.clock.items():
            if self.seen.get(k, 0) < v:
                self.seen[k] = v

    def issue(self, ins):
        self.count += 1
        ins.then_inc(self.sem, 1)
        clock = dict(self.seen)
        clock[self.key] = self.count
        return Tok(self.sem, self.key, self.count, clock)


class Lane:
    def __init__(self, nc, name):
        self.key = "l_" + name
        self.sem = nc.alloc_semaphore(name=self.key)
        self.count = 0
        self.last = None


class Buf:
    def __init__(self, name):
        self.name = name
        self.w = {}
        self.r = {}
        self.pr = {}


def _pre(eng, reads, writes, disjoint):
    for b in reads:
        for t in list(b.w.values()):
            eng.wait(t)
    for b in writes:
        for t in list(b.pr.values()):
            eng.wait(t)
        for t in list(b.r.values()):
            eng.wait(t)
        if (not disjoint) or b.r:
            for t in list(b.w.values()):
                eng.wait(t)


def _post(tok, reads, writes):
    for b in reads:
        b.r[tok.key] = tok
    for b in writes:
        if b.r:
            b.pr = b.r
            b.r = {}
            b.w = {tok.key: tok}
        else:
            b.w[tok.key] = tok


def op(eng, fns, reads=(), writes=(), disjoint=False):
    _pre(eng, reads, writes, disjoint)
    ins = None
    for f in fns:
        ins = f()
    tok = eng.issue(ins)
    _post(tok, reads, writes)
    return tok


def dma(q, lane, out, in_, reads=(), writes=(), disjoint=False, **kw):
    _pre(q, reads, writes, disjoint)
    q.wait(lane.last)
    ins = q.eng.dma_start(out=out, in_=in_, **kw)
    lane.count += 16
    ins.then_inc(lane.sem, 16)
    tok = Tok(lane.sem, lane.key, lane.count, dict(q.seen))
    lane.last = tok
    _post(tok, reads, writes)
    return tok


def build(NB=NBLK, NL=DEPTH):
    nc = bass.Bass("TRN2", target_bir_lowering=False)

    def din(name, shape):
        return nc.dram_tensor(name, list(shape), F32, kind="ExternalInput").ap()

    def dout(name, shape):
        return nc.dram_tensor(name, list(shape), F32, kind="ExternalOutput").ap()

    x_prompt = din("x_prompt", [SEQ, D])
    x_sample = din("x_sample", [TS, D])
    cache_k = din("cache_k", [DEPTH, PAST, DA])
    cache_v = din("cache_v", [DEPTH, PAST, DA])
    cache_logf = din("cache_logf", [DEPTH, PAST, H])
    state_pool = din("state_pool", [DEPTH, 15, DP])
    meta_tokens = din("meta_tokens", [NMETA, D])
    w_in = din("w_in", [DEPTH, D, DIN])
    b_f = din("b_f", [DEPTH, H])
    w_pool_grp = din("w_pool_grp", [DEPTH, 4, 128, 128])
    pool_scale = din("pool_scale", [DEPTH, DP])
    w_attn_br = din("w_attn_br", [DEPTH, DA, D])
    w_pool_br = din("w_pool_br", [DEPTH, DP, D])
    w_out = din("w_out", [DEPTH, D, D])
    ln1_g = din("ln1_g", [DEPTH, D])
    ln1_b = din("ln1_b", [DEPTH, D])
    w_up = din("w_up", [DEPTH, D, DFF])
    w_down = din("w_down", [DEPTH, DFF, D])
    ln2_g = din("ln2_g", [DEPTH, D])
    ln2_b = din("ln2_b", [DEPTH, D])
    consts = din("consts", [128, 576])

    y_prompt = dout("y_prompt", [SEQ, D])
    y_sample = dout("y_sample", [TS, D])
    k_prompt = dout("k_prompt", [DEPTH, L_TOT, DA])
    v_prompt = dout("v_prompt", [DEPTH, L_TOT, DA])
    logf_prompt = dout("logf_prompt", [DEPTH, L_TOT, H])
    pool_prompt = dout("pool_prompt", [DEPTH, 15, DP])
    k_sample = dout("k_sample", [DEPTH, TS, DA])
    v_sample = dout("v_sample", [DEPTH, TS, DA])
    logf_sample = dout("logf_sample", [DEPTH, TS, H])
    pool_sample = dout("pool_sample", [DEPTH, 15, DP])

    WS = nc.dram_tensor("ws", [DEPTH, NCH, 128, 4096], BF16, kind="Internal").ap()
    KVS = nc.dram_tensor("kvs", [DEPTH, NBLK, 128, 4, KVW], BF16, kind="Internal").ap()
    XS = nc.dram_tensor("xs", [SEQ + NMETA + TS, D], F32, kind="Internal").ap()

    PE = Eng(nc, "pe", nc.tensor, False)
    ACT = Eng(nc, "act", nc.scalar, True)
    DVE = Eng(nc, "dve", nc.vector, True)
    POOL = Eng(nc, "pool", nc.gpsimd, True)
    SP = Eng(nc, "sp", nc.sync, False)

    def sb(name, shape, dt=F32):
        return nc.alloc_sbuf_tensor(name, list(shape), dt)

    CONST = sb("const", [128, 576])
    TRIF = CONST[:, 0:128]
    ONESF = CONST[:, 128:256]
    IDF = CONST[:, 256:384]
    INVC = CONST[:, 384:448]
    SELF = CONST[:, 448:576]
    CB16 = sb("cb16", [128, 384], BF16)
    TRIB = CB16[:, 0:128]
    IDB = CB16[:, 128:256]
    SELB = CB16[:, 256:384]
    BFB = sb("bfb", [128, DEPTH * H])
    PSC = sb("psc", [128, DEPTH * 4])
    LNP = sb("lnp", [128, 4, D])
    XT = sb("xt", [128, 4, D])
    TMPN = sb("tmpn", [128, 4, D])
    XBN = sb("xbn", [128, 4, D], BF16)
    XTRA = sb("xtra", [128, 8, 512], BF16)
    XTRB = sb("xtrb", [128, 8, 512], BF16)
    QT = sb("qt", [128, 2, 4, 512], BF16)
    KVST = sb("kvst", [128, 4, KVW], BF16)
    R = sb("r", [128, 4, 4096], BF16)
    UB = sb("ub", [128, 4, 528])
    UM = sb("um", [128, 4, 32])
    US = sb("us", [128, 4, 48])
    PT1 = sb("pt1", [128, 528])
    PT2 = sb("pt2", [128, 528])
    POOLED = sb("pooled", [128, 4, 512], BF16)
    PTP = sb("ptp", [128, 4, 512], BF16)
    MG = sb("mg", [128, 2, 512])
    RL = sb("rl", [128, 2, 512])
    WR = sb("wr", [128, NW, 4096], BF16)
    RING = sb("ring", [128, NR, KVW], BF16)
    PTL = sb("ptl", [128, NP, 512], BF16)
    RRB = sb("rrb", [128, 2, 512], BF16)
    CT = sb("ct", [128, 64, H])
    BIAS = sb("bias", [128, 64, H])
    CSM = sb("csm", [128, 16, H])
    BSM = sb("bsm", [128, 16, H])
    LG = sb("lg", [128, 8, H])
    LGT = sb("lgt", [128, 8, H])
    CBT = sb("cbt", [128, 9, H])
    STT = sb("stt", [128, 4, 2, 6])
    MV = sb("mv", [128, 4, 8])
    POUT = sb("pout", [16, 512])
    SPT = POUT
    KM = sb("km", [128, 4, 16], BF16)
    VM = sb("vm", [16, 4, SLAB], BF16)
    VMS = sb("vms", [16, 2, SLAB], BF16)
    print("sbuf bytes remaining/partition:", nc.sbuf_bytes_remaining() if callable(nc.sbuf_bytes_remaining) else nc.sbuf_bytes_remaining)

    PSALL = nc.alloc_psum_tensor("psall", [128, 4096], F32)
    PS = [PSALL[:, 512 * i:512 * (i + 1)] for i in range(8)]
    PSB = [Buf(f"ps{i}") for i in range(8)]

    names = ["const", "cb16", "bfb", "psc", "lnp", "xtra", "xtrb", "qt", "kvst", "ub", "um", "us", "pt1", "pt2",
             "pooled", "ptp", "ct", "bias", "csm", "bsm", "lg", "lgt", "cbt", "pout", "km", "vm", "carry", "rrb", "vms0", "vms1"]
    B = {n: Buf(n) for n in names}
    XTb = [Buf(f"xt{t}") for t in range(4)]
    TMPNb = [Buf(f"tmpn{t}") for t in range(4)]
    XBNb = [Buf(f"xbn{t}") for t in range(4)]
    Rb = [Buf(f"r{t}") for t in range(4)]
    MGb = [Buf(f"mg{t}") for t in range(2)]
    RLb = [Buf(f"rl{t}") for t in range(2)]
    KOUT, KOUTb = MG, MGb
    VOUT, VOUTb = RL, RLb
    WRb = [Buf(f"wr{t}") for t in range(NW)]
    RINGb = [Buf(f"ring{t}") for t in range(NR)]
    PTLb = [Buf(f"ptl{t}") for t in range(NP)]
    STTb = [Buf(f"stt{t}") for t in range(4)]
    MVb = [Buf(f"mv{t}") for t in range(4)]
    WSb = [[Buf(f"ws{l}_{c}") for c in range(NCH)] for l in range(DEPTH)]
    KVSb = [[Buf(f"kvs{l}_{i}") for i in range(NBLK)] for l in range(DEPTH)]
    XSb = [Buf(f"xs{i}") for i in range(NBLK + 1)]

    WL = [Lane(nc, f"w{t}") for t in range(NW)]
    RLn = [Lane(nc, f"r{t}") for t in range(NR)]
    XL = [Lane(nc, f"x{t}") for t in range(4)]
    XBL = [Lane(nc, f"xb{t}") for t in range(4)]
    OL = [Lane(nc, f"o{t}") for t in range(8)]
    CLn = [Lane(nc, f"c{t}") for t in range(8)]
    ML = [Lane(nc, f"m{t}") for t in range(4)]
    rr_ctr = {"o": 0, "c": 0, "m": 0, "g": 0, "p": 0, "rl": 0, "mg": 0, "xb": 0, "ko": 0, "vo": 0, "yst": 0,
              "tmpn": 0, "ring": 0}

    def nxt(k, n):
        v = rr_ctr[k]
        rr_ctr[k] = (v + 1) % n
        return v

    def olane():
        return OL[nxt("o", 8)]

    def mlane():
        return ML[nxt("m", 4)]

    V = nc.vector
    A = nc.scalar
    G = nc.gpsimd
    T = nc.tensor

    dma(SP, mlane(), CONST[:], consts, writes=[B["const"]])
    dma(SP, mlane(), BFB[:], b_f.rearrange("l h -> (l h)").partition_broadcast(128), writes=[B["bfb"]])
    with nc.allow_non_contiguous_dma(reason="tiny one-time param load"):
        dma(SP, mlane(), PSC[:].rearrange("c (l g) -> c l g", l=DEPTH),
            pool_scale.rearrange("l (g c) -> c l g", g=4), writes=[B["psc"]])
    op(POOL, [lambda: G.tensor_copy(out=TRIB, in_=TRIF), lambda: G.tensor_copy(out=IDB, in_=IDF),
              lambda: G.tensor_copy(out=SELB, in_=SELF)],
       reads=[B["const"]], writes=[B["cb16"]])
    op(POOL, [lambda: G.memset(QT[:], 0.0)], writes=[B["qt"]])
    op(POOL, [lambda: G.memset(RRB[:], 0.0)], writes=[B["rrb"]])
    op(POOL, [lambda: G.memset(RL[:], 0.0)], writes=[RLb[0], RLb[1]])
    op(POOL, [lambda: G.memset(KVST[:], 1.0)], writes=[B["kvst"]])
    for s in range(NR):
        op(POOL, [lambda s=s: G.memset(RING[:, s, :], 1.0)], writes=[RINGb[s]])
    op(POOL, [lambda: G.memset(VM[:], 1.0)], writes=[B["vm"]])
    op(POOL, [lambda: G.memset(UM[:], 0.0)], writes=[B["um"]])
    op(POOL, [lambda: G.memset(US[:], 0.0)], writes=[B["us"]])
    op(POOL, [lambda: G.memset(UB[:], 0.0)], writes=[B["ub"]])

    def chunk_src(l, c):
        def kcv(ap):
            return ap.rearrange("(kc p) n -> p kc n", p=128)
        if c == 0:
            return kcv(w_in[l][:, OFF_F:OFF_F + 8])
        if c == 1:
            return kcv(w_in[l][:, OFF_K:OFF_K + 512])
        if c == 2:
            return kcv(w_in[l][:, OFF_V:OFF_V + 512])
        if c == 3:
            return kcv(w_in[l][:, OFF_Q:OFF_Q + 512])
        if c == 4:
            return kcv(w_in[l][:, OFF_U:OFF_U + 512])
        if c in (5, 6):
            o = OFF_GA + 512 * (c - 5)
            return kcv(w_in[l][:, o:o + 512])
        if c in (7, 8):
            o = OFF_GP + 512 * (c - 7)
            return kcv(w_in[l][:, o:o + 512])
        if c == 9:
            return w_pool_grp[l].rearrange("g c e -> c g e")
        if c in (10, 12):
            j = (c - 10) // 2
            return kcv(w_attn_br[l][:, 512 * j:512 * j + 512])
        if c in (11, 13):
            j = (c - 11) // 2
            return kcv(w_pool_br[l][:, 512 * j:512 * j + 512])
        if c in (14, 15):
            j = c - 14
            return kcv(w_out[l][:, 512 * j:512 * j + 512])
        if 16 <= c < 24:
            j = c - 16
            return kcv(w_up[l][:, 512 * j:512 * j + 512])
        j = c - 24
        half, g = j // 4, j % 4
        return kcv(w_down[l][1024 * g:1024 * g + 1024, 512 * half:512 * half + 512])

    def chunk_shape(c):
        if c == 0:
            return (128, 8, 8)
        if c == 9:
            return (128, 4, 128)
        if c in (10, 11, 12, 13):
            return (128, 4, 512)
        return (128, 8, 512)

    def ws_view(l, c):
        P_, A_, B_ = chunk_shape(c)
        return WS[l, c][0:P_, 0:A_ * B_].rearrange("p (a b) -> p a b", a=A_)

    def wr_view(slot, c):
        P_, A_, B_ = chunk_shape(c)
        return WR[0:P_, slot, 0:A_ * B_].rearrange("p (a b) -> p a b", a=A_)

    def emit_casts(l):
        for c in range(NCH):
            dma(POOL, CLn[nxt("c", 8)], ws_view(l, c), chunk_src(l, c), writes=[WSb[l][c]])

    wseq = []
    for l in range(NL):
        for _b in range(NB + 1):
            for c in range(NCH):
                wseq.append((l, c))
    wstate = {"next_load": 0, "next_use": 0}

    def w_prefetch(upto):
        while wstate["next_load"] < min(upto, len(wseq)):
            i = wstate["next_load"]
            l, c = wseq[i]
            slot = i % NW
            dma(SP, WL[slot], wr_view(slot, c), ws_view(l, c), reads=[WSb[l][c]], writes=[WRb[slot]])
            wstate["next_load"] += 1

    def w_get(l, c, hold=0):
        i = wstate["next_use"]
        assert wseq[i] == (l, c), (wseq[i], l, c)
        w_prefetch(i + NW - hold)
        wstate["next_use"] += 1
        slot = i % NW
        return wr_view(slot, c), WRb[slot]

    ring_seq = []
    for l_ in range(NL):
        for hp_ in range(4):
            for kb_ in range(2):
                ring_seq.append(("sample", l_, -1, hp_, kb_))
        for i_ in range(NB):
            for hp_ in range(4):
                for kb_ in range(i_):
                    ring_seq.append(("main", l_, i_, hp_, kb_))
    ring_state = {"next_load": 0, "next_use": 0}
    published = set()
    RING_LA = NR - 2

    def ring_get(kind, l, i, hp, kb):
        n = ring_state["next_use"]
        assert ring_seq[n] == (kind, l, i, hp, kb), (ring_seq[n], kind, l, i, hp, kb)
        while ring_state["next_load"] < len(ring_seq) and ring_state["next_load"] <= n + RING_LA:
            m = ring_state["next_load"]
            e = ring_seq[m]
            if m > n and (e[0] == "sample" or (e[1], e[4]) not in published):
                break
            if e[0] == "main":
                sl = m % NR
                dma(SP, RLn[sl], RING[:, sl, :], KVS[e[1], e[4]][:, e[3], :], reads=[KVSb[e[1]][e[4]]], writes=[RINGb[sl]])
            ring_state["next_load"] += 1
        ring_state["next_use"] += 1
        return n % NR

    def mm(out, lhsT, rhs, start, stop):
        return lambda: T.matmul(out, lhsT=lhsT, rhs=rhs, start=start, stop=stop)

    gen_banks = [0, 1, 2, 3]

    def gbank():
        return gen_banks[nxt("g", 4)]

    def transposes(tl_list, src_fn, dst, dstbuf):
        for tl in tl_list:
            col0, n = tl["col0"], tl["n"]
            src, sbuf_ = src_fn(tl)
            bk = gbank()
            pb = PS[bk][:].bitcast(BF16)
            fns = [(lambda kc=kc, n=n, src=src, pb=pb: T.transpose(out=pb[:, kc * 128:kc * 128 + n],
                                                                    in_=src[:, kc * 128:(kc + 1) * 128],
                                                                    identity=IDB[0:n, 0:n])) for kc in range(8)]
            op(PE, fns, reads=[sbuf_, B["cb16"]], writes=[PSB[bk]])
            op(DVE, [lambda n=n, col0=col0, pb=pb: V.tensor_copy(
                out=dst[:, :, col0:col0 + n],
                in_=pb.rearrange("p (k t) -> p k t", k=8)[:, :, 0:n])],
               reads=[PSB[bk]], writes=[dstbuf], disjoint=True)

    def layer_norm_multi(tls, gi, bi, in_place, after=None, kbase=0):
        K_ = len(tls)
        for k, tl in enumerate(tls, kbase):
            t, n = tl["t"], tl["n"]
            yield op(DVE, [lambda: V.bn_stats(out=STT[0:n, k, 0, :], in_=XT[0:n, t, 0:512]),
                           lambda: V.bn_stats(out=STT[0:n, k, 1, :], in_=XT[0:n, t, 512:1024])],
                     reads=[XTb[t]], writes=[STTb[k]])
            yield op(DVE, [lambda: V.bn_aggr(out=MV[0:n, k, 0:2], in_=STT[0:n, k].rearrange("p a b -> p (a b)"))],
                     reads=[STTb[k]], writes=[MVb[k]])
        for k, tl in enumerate(tls, kbase):
            t, n = tl["t"], tl["n"]
            yield op(ACT, [lambda: A.activation(out=MV[0:n, k, 2:3], in_=MV[0:n, k, 1:2], func=AF.Ln, bias=LN_EPS, scale=1.0)],
                     reads=[MVb[k]], writes=[MVb[k]])
            yield op(ACT, [lambda: A.activation(out=MV[0:n, k, 3:4], in_=MV[0:n, k, 2:3], func=AF.Exp, scale=-0.5)],
                     reads=[MVb[k]], writes=[MVb[k]])
        for k, tl in enumerate(tls, kbase):
            t, n = tl["t"], tl["n"]
            yield op(DVE, [lambda: V.scalar_tensor_tensor(out=MV[0:n, k, 4:5], in0=MV[0:n, k, 0:1], scalar=-1.0,
                                                          in1=MV[0:n, k, 3:4], op0=ALU.mult, op1=ALU.mult)],
                     reads=[MVb[k]], writes=[MVb[k]])
        for k, tl in enumerate(tls, kbase):
            t, n = tl["t"], tl["n"]
            yield op(ACT, [lambda: A.activation(out=TMPN[0:n, k, :], in_=XT[0:n, t, :], func=AF.Identity,
                                                scale=MV[0:n, k, 3:4], bias=MV[0:n, k, 4:5])],
                     reads=[MVb[k], XTb[t]], writes=[TMPNb[k]])
        for k, tl in enumerate(tls, kbase):
            t, n = tl["t"], tl["n"]
            yield op(DVE, [lambda: V.tensor_tensor(out=TMPN[0:n, k, :], in0=TMPN[0:n, k, :], in1=LNP[0:n, gi, :], op=ALU.mult)],
                     reads=[B["lnp"], TMPNb[k]], writes=[TMPNb[k]])
        for k, tl in enumerate(tls, kbase):
            t, n = tl["t"], tl["n"]
            if in_place:
                yield op(POOL, [lambda: G.tensor_tensor(out=XT[0:n, t, :], in0=TMPN[0:n, k, :], in1=LNP[0:n, bi, :], op=ALU.add)],
                         reads=[B["lnp"], TMPNb[k]], writes=[XTb[t]])
            else:
                yield op(POOL, [lambda: G.tensor_tensor(out=TMPN[0:n, k, :], in0=TMPN[0:n, k, :], in1=LNP[0:n, bi, :], op=ALU.add)],
                         reads=[B["lnp"], TMPNb[k]], writes=[TMPNb[k]])
                if after is not None:
                    yield after(k, tl)
        if in_place:
            for k, tl in enumerate(tls, kbase):
                t, n = tl["t"], tl["n"]
                yield op(ACT, [lambda: A.activation(out=XBN[0:n, t, :], in_=XT[0:n, t, :], func=AF.Copy)],
                         reads=[XTb[t]], writes=[XBNb[t]])

    def scale_slab(dst, src, nk, wA, wB, rbufs, wbuf):
        op(DVE, [lambda: V.tensor_scalar(out=dst[0:nk, 0:65], in0=src[0:nk, 0:65], scalar1=wA, scalar2=None, op0=ALU.mult),
                 lambda: V.tensor_scalar(out=dst[0:nk, 66:162], in0=src[0:nk, 66:162], scalar1=wB, scalar2=None, op0=ALU.mult)],
           reads=rbufs, writes=[wbuf])

    def scale_slab4(reg, W4, hA, hB, rbufs, wbuf):
        r4 = reg.rearrange("p (t c) -> p t c", t=4)
        op(DVE, [lambda: V.tensor_tensor(out=r4[:, :, 0:65], in0=r4[:, :, 0:65],
                                         in1=W4[:, :, hA:hA + 1].broadcast_to([128, 4, 65]), op=ALU.mult),
                 lambda: V.tensor_tensor(out=r4[:, :, 66:162], in0=r4[:, :, 66:162],
                                         in1=W4[:, :, hB:hB + 1].broadcast_to([128, 4, 96]), op=ALU.mult)],
           reads=rbufs, writes=[wbuf])

    def attention(q0, nq, ktile_fn):
        ATT = R[:, 3, 0:2048].rearrange("p (h t) -> p h t", h=4)
        pending = [None]
        for hp in range(4):
            tiles = ktile_fn(hp)
            ob = [4, 5] if hp % 2 == 0 else [6, 7]
            nt = len(tiles)
            sb_of = {}

            def emit_S(idx):
                if callable(tiles[idx]):
                    tiles[idx] = tiles[idx]()
                kt = tiles[idx]
                nk, qlo = kt["nk"], kt["qlo"]
                pr = idx % 2
                for hh in range(2):
                    bk = 2 * pr + hh
                    op(PE, [mm(PS[bk][0:nk, qlo:nq], kt["K"], QT[:, hh, hp, q0 + qlo:q0 + nq], True, True)],
                       reads=[B["qt"]] + kt["bufs"], writes=[PSB[2 * pr], PSB[2 * pr + 1]])
                sb_of[idx] = pr

            emit_S(0)
            if nt > 1:
                emit_S(1)
            for idx in range(nt):
                kt = tiles[idx]
                nk, qlo = kt["nk"], kt["qlo"]
                pr = sb_of[idx]
                pp = nxt("p", 3)
                op(ACT, [lambda: A.activation(
                    out=PTL[0:nk, 2 * pp:2 * pp + 2, qlo:nq],
                    in_=PSALL[0:nk, 1024 * pr:1024 * pr + 1024].rearrange("k (h c) -> k h c", h=2)[:, :, qlo:nq],
                    func=AF.Exp)],
                   reads=[PSB[2 * pr], PSB[2 * pr + 1]], writes=[PTLb[2 * pp]])
                if kt["diag"]:
                    w = min(nk, nq - qlo)
                    op(POOL, [lambda: G.tensor_tensor(
                        out=PTL[0:nk, 2 * pp:2 * pp + 2, qlo:qlo + w], in0=PTL[0:nk, 2 * pp:2 * pp + 2, qlo:qlo + w],
                        in1=TRIB[0:nk, 0:w].unsqueeze(1).broadcast_to([nk, 2, w]), op=ALU.mult)],
                       reads=[PTLb[2 * pp], B["cb16"]], writes=[PTLb[2 * pp]])
                if pending[0] is not None and idx == min(5, nt - 1):
                    pending[0](2 * pr)
                    pending[0] = None
                if idx + 2 < nt:
                    emit_S(idx + 2)
                for hh in range(2):
                    Vap = kt["V"][0:nk, 0:128] if hh == 0 else kt["V"][0:nk, 34:162]
                    op(PE, [mm(PS[ob[hh]][0:128, qlo:nq], Vap, PTL[0:nk, 2 * pp + hh, qlo:nq], idx == 0, idx == nt - 1)],
                       reads=[PTLb[2 * pp]] + kt["bufs"], writes=[PSB[ob[hh]]])
            oA, oB = ob
            LNS = 20.72326583694641
            op(ACT, [lambda: A.activation(out=RL[64:65, 0, 0:nq], in_=PS[oA][64:65, 0:nq], func=AF.Ln, scale=1.0e9),
                     lambda: A.activation(out=RL[32:33, 0, 0:nq], in_=PS[oB][32:33, 0:nq], func=AF.Ln, scale=1.0e9)],
               reads=[PSB[oA], PSB[oB]], writes=[RLb[0]])
            op(ACT, [lambda: A.activation(out=RL[64:65, 0, 0:nq], in_=RL[64:65, 0, 0:nq], func=AF.Exp, scale=-1.0, bias=LNS),
                     lambda: A.activation(out=RL[32:33, 0, 0:nq], in_=RL[32:33, 0, 0:nq], func=AF.Exp, scale=-1.0, bias=LNS)],
               reads=[RLb[0]], writes=[RLb[0]])
            op(DVE, [lambda: V.tensor_copy(out=RRB[0:65, 0, 0:nq], in_=RL[0:65, 0, 0:nq])],
               reads=[RLb[0]], writes=[B["rrb"]])
            op(DVE, [lambda: V.tensor_tensor(out=RRB[0:65, 1, 0:nq], in0=RL[0:65, 0, 0:nq], in1=RRB[0:65, 0, 0:nq],
                                             op=ALU.subtract)],
               reads=[RLb[0], B["rrb"]], writes=[B["rrb"]])

            def part2(bc, hp=hp, oA=oA, oB=oB):
                op(PE, [mm(PS[bc][:, 0:nq], SELB, RRB[:, 0, 0:nq], True, False),
                        mm(PS[bc][:, 0:nq], SELB, RRB[:, 1, 0:nq], False, True)],
                   reads=[B["rrb"], B["cb16"]], writes=[PSB[bc]])
                op(ACT, [lambda: A.activation(out=MG[:, 0, 0:nq], in_=PS[bc][:, 0:nq], func=AF.Copy)],
                   reads=[PSB[bc]], writes=[MGb[0]])
                op(DVE, [lambda: V.tensor_tensor(out=ATT[0:64, hp, q0:q0 + nq], in0=PS[oA][0:64, 0:nq],
                                                 in1=MG[0:64, 0, 0:nq], op=ALU.mult)],
                   reads=[PSB[oA], MGb[0]], writes=[Rb[3]], disjoint=True)
                op(DVE, [lambda: V.tensor_tensor(out=ATT[64:128, hp, q0:q0 + nq], in0=PS[oB][64:128, 0:nq],
                                                 in1=MG[64:128, 0, 0:nq], op=ALU.mult)],
                   reads=[PSB[oB], MGb[0]], writes=[Rb[3]], disjoint=True)
            pending[0] = part2
        pending[0](gbank())

    def stage_xload(blk):
        l = blk["l"]
        Tn = blk["T"]
        tiles = blk["tiles"]
        for tl in tiles:
            t, n = tl["t"], tl["n"]
            dma(SP, XL[t], XT[0:n, t, :], tl["xsrc"], reads=tl["xsbuf"], writes=[XTb[t]])

    def stage_xbf(blk):
        l = blk["l"]
        Tn = blk["T"]
        tiles = blk["tiles"]
        for tl in tiles:
            t, n = tl["t"], tl["n"]
            dma(POOL, XBL[t], XBN[0:n, t, :], tl["xsrc"], reads=tl["xsbuf"], writes=[XBNb[t]])

    def stage_xT(blk):
        l = blk["l"]
        Tn = blk["T"]
        tiles = blk["tiles"]
        transposes(tiles, lambda tl: (XBN[0:tl["n"], tl["t"], :], XBNb[tl["t"]]), XTRA, B["xtra"])

    def stage_P1(blk):
        l = blk["l"]
        Tn = blk["T"]
        tiles = blk["tiles"]
        WF, WFb = w_get(l, 0)
        for tl in tiles:
            t, n, col0 = tl["t"], tl["n"], tl["col0"]
            yield op(PE, [mm(PS[3][0:n, 8 * t:8 * t + 8], XTRA[:, kc, col0:col0 + n], WF[:, kc, 0:8], kc == 0, kc == 7)
                    for kc in range(8)], reads=[B["xtra"], WFb], writes=[PSB[3]])
        ntl = len(tiles)
        nmax = max(tl["n"] for tl in tiles)
        yield op(DVE, [lambda: V.tensor_tensor(out=LGT[0:nmax, 0:ntl, :],
                                         in0=PS[3][0:nmax, 0:8 * ntl].rearrange("p (t h) -> p t h", h=H),
                                         in1=BFB[0:nmax, l * H:(l + 1) * H].unsqueeze(1).broadcast_to([nmax, ntl, H]),
                                         op=ALU.add)],
           reads=[PSB[3], B["bfb"]], writes=[B["lgt"]])
        yield op(DVE, [lambda: V.tensor_scalar(out=LGT[0:nmax, 0:ntl, :], in0=LGT[0:nmax, 0:ntl, :], scalar1=-1.0, scalar2=60.0,
                                         op0=ALU.mult, op1=ALU.min)],
           reads=[B["lgt"]], writes=[B["lgt"]])
        yield op(ACT, [lambda: A.activation(out=LGT[0:nmax, 0:ntl, :], in_=LGT[0:nmax, 0:ntl, :], func=AF.Exp)],
           reads=[B["lgt"]], writes=[B["lgt"]])
        yield op(ACT, [lambda: A.activation(out=LGT[0:nmax, 0:ntl, :], in_=LGT[0:nmax, 0:ntl, :], func=AF.Ln, bias=1.0, scale=1.0)],
           reads=[B["lgt"]], writes=[B["lgt"]])
        yield op(DVE, [lambda: V.tensor_scalar(out=LG[0:nmax, 0:ntl, :], in0=LGT[0:nmax, 0:ntl, :], scalar1=-1.0, scalar2=None,
                                         op0=ALU.mult)],
           reads=[B["lgt"]], writes=[B["lg"]])
        for tl in tiles:
            t, n = tl["t"], tl["n"]
            with nc.allow_non_contiguous_dma(reason="logf rows are 32B"):
                yield dma(SP, olane(), tl["lout"], LG[0:n, t, :], reads=[B["lg"]])

        WK, WKb = w_get(l, 1)
        for tl in tiles:
            t, n, col0 = tl["t"], tl["n"], tl["col0"]
            bk = gbank()
            yield op(PE, [mm(PS[bk][0:n, 0:512], XTRA[:, kc, col0:col0 + n], WK[:, kc, :], kc == 0, kc == 7) for kc in range(8)],
               reads=[B["xtra"], WKb], writes=[PSB[bk]])
            ko = nxt("ko", 2)
            yield op(ACT, [lambda bk=bk, ko=ko, n=n: A.activation(out=KOUT[0:n, ko, :], in_=PS[bk][0:n, 0:512], func=AF.Copy)],
               reads=[PSB[bk]], writes=[KOUTb[ko]])
            yield dma(SP, olane(), tl["kout"], KOUT[0:n, ko, :], reads=[KOUTb[ko]])
        for hp in range(4):
            bk = gbank()
            yield op(PE, [mm(PS[bk][:, 0:Tn], WK[:, kc, 128 * hp:128 * hp + 128], XTRA[:, kc, 0:Tn], kc == 0, kc == 7)
                    for kc in range(8)], reads=[B["xtra"], WKb], writes=[PSB[bk]])
            yield op(DVE, [lambda bk=bk, hp=hp: V.tensor_copy(out=KVST[:, hp, 0:Tn], in_=PS[bk][:, 0:Tn])],
               reads=[PSB[bk]], writes=[B["kvst"]], disjoint=(hp > 0))
        WV, WVb = w_get(l, 2)
        for tl in tiles:
            t, n, col0 = tl["t"], tl["n"], tl["col0"]
            bk = gbank()
            yield op(PE, [mm(PS[bk][0:n, 0:512], XTRA[:, kc, col0:col0 + n], WV[:, kc, :], kc == 0, kc == 7) for kc in range(8)],
               reads=[B["xtra"], WVb], writes=[PSB[bk]])
            vo = nxt("vo", 2)
            yield op(DVE, [lambda bk=bk, vo=vo, n=n: V.tensor_copy(out=VOUT[0:n, vo, :], in_=PS[bk][0:n, 0:512])],
               reads=[PSB[bk]], writes=[VOUTb[vo]])
            yield dma(SP, olane(), tl["vout"], VOUT[0:n, vo, :], reads=[VOUTb[vo]])
            vt = tl["vt"]
            yield op(POOL, [lambda n=n, vt=vt: G.memset(KVST[0:n, :, 512 + SLAB * vt + 64:512 + SLAB * vt + 98], 1.0)],
                     writes=[B["kvst"]], disjoint=True)
            for hh in range(2):
                base = 512 + SLAB * vt + 98 * hh
                yield op(POOL, [lambda vo=vo, n=n, base=base, hh=hh: G.tensor_copy(
                    out=KVST[0:n, :, base:base + 64],
                    in_=VOUT[0:n, vo, :].rearrange("p (a b c) -> p a b c", a=4, b=2)[:, :, hh, :])],
                   reads=[VOUTb[vo]], writes=[B["kvst"]], disjoint=True)
        WQ, WQb = w_get(l, 3)
        for hp in range(4):
            bk = gbank()
            yield op(PE, [mm(PS[bk][:, 0:Tn], WQ[:, kc, 128 * hp:128 * hp + 128], XTRA[:, kc, 0:Tn], kc == 0, kc == 7)
                    for kc in range(8)], reads=[B["xtra"], WQb], writes=[PSB[bk]])
            yield op(ACT, [lambda bk=bk, hp=hp: A.activation(out=QT[0:64, 0, hp, 0:Tn], in_=PS[bk][0:64, 0:Tn], func=AF.Copy, scale=0.125),
                     lambda bk=bk, hp=hp: A.activation(out=QT[64:128, 1, hp, 0:Tn], in_=PS[bk][64:128, 0:Tn], func=AF.Copy, scale=0.125)],
               reads=[PSB[bk]], writes=[B["qt"]], disjoint=(hp > 0))

        blk["cumsum"](l)

        blk["publish"](l)


    def stage_P2(blk):
        l = blk["l"]
        Tn = blk["T"]
        tiles = blk["tiles"]
        WU, WUb = w_get(l, 4)
        for g in range(4):
            bk = gbank()
            yield op(PE, [mm(PS[bk][:, 0:Tn], WU[:, kc, 128 * g:128 * g + 128], XTRA[:, kc, 0:Tn], kc == 0, kc == 7)
                    for kc in range(8)], reads=[B["xtra"], WUb], writes=[PSB[bk]])
            for sg in blk["segs"]:
                ub, ubuf, c0, n = sg["U"], sg["Ubuf"], sg["col0"], sg["n"]
                yield op(ACT, [lambda bk=bk, g=g, ub=ub, c0=c0, n=n: A.activation(out=ub[:, g, 16:16 + n], in_=PS[bk][:, c0:c0 + n],
                                                                                func=AF.Copy)],
                   reads=[PSB[bk]], writes=[ubuf], disjoint=(g > 0))
        for sg in blk["segs"]:
            ub, ubuf, c0, n = sg["U"], sg["Ubuf"], sg["col0"], sg["n"]
            W_ = 16 + n
            if sg.get("pre"):
                sg["pre"](l)
            for g in range(4):
                w = 2 << g
                src = ub[:, g, :]
                cur = src
                curb = ubuf
                tmps = [(PT1, B["pt1"]), (PT2, B["pt2"])]
                sh = 1
                lo = 0
                for s in range(g + 1):
                    dst, dstb = tmps[s % 2]
                    lo2 = lo + sh
                    yield op(POOL, [lambda dst=dst, cur=cur, lo2=lo2, sh=sh, W_=W_: G.tensor_tensor(
                        out=dst[:, lo2:W_], in0=cur[:, lo2:W_], in1=cur[:, lo2 - sh:W_ - sh], op=ALU.add)],
                       reads=[curb], writes=[dstb])
                    cur, curb = dst[:, :] if False else dst, dstb
                    lo = lo2
                    sh *= 2
                if sg["kind"] == "meta":
                    yield op(DVE, [lambda cur=cur, g=g, n=n: V.tensor_tensor(out=cur[:, 16:16 + n], in0=cur[:, 16:16 + n],
                                                                       in1=INVC[:, 16 * g:16 * g + 16], op=ALU.mult)],
                       reads=[curb, B["const"]], writes=[curb])
                    yield op(DVE, [lambda cur=cur, g=g, n=n, c0=c0, src=src: V.tensor_tensor(
                        out=POOLED[:, g, c0:c0 + n], in0=cur[:, 16:16 + n], in1=src[:, 16:16 + n], op=ALU.subtract)],
                       reads=[curb, ubuf], writes=[B["pooled"]], disjoint=True)
                else:
                    yield op(DVE, [lambda cur=cur, g=g, n=n, c0=c0, src=src, w=w: V.scalar_tensor_tensor(
                        out=POOLED[:, g, c0:c0 + n], in0=cur[:, 16:16 + n], scalar=1.0 / w, in1=src[:, 16:16 + n],
                        op0=ALU.mult, op1=ALU.subtract)],
                       reads=[curb, ubuf], writes=[B["pooled"]], disjoint=True)
            if sg.get("post"):
                sg["post"](l)
        SG = [R[:, 0, :].rearrange("p (k t) -> p k t", k=8), R[:, 1, :].rearrange("p (k t) -> p k t", k=8)]
        for gi in range(2):
            for j in range(2):
                Wg, Wgb = w_get(l, 5 + 2 * gi + j)
                for cc in range(4):
                    oc = 4 * j + cc
                    bk = gbank()
                    yield op(PE, [mm(PS[bk][:, 0:Tn], Wg[:, kc, 128 * cc:128 * cc + 128], XTRA[:, kc, 0:Tn], kc == 0, kc == 7)
                            for kc in range(8)], reads=[B["xtra"], Wgb], writes=[PSB[bk]])
                    yield op(ACT, [lambda bk=bk, gi=gi, oc=oc: A.activation(out=SG[gi][:, oc, 0:Tn], in_=PS[bk][:, 0:Tn],
                                                                     func=AF.Sigmoid)],
                       reads=[PSB[bk]], writes=[Rb[gi]], disjoint=(oc > 0))

        WG, WGb = w_get(l, 9)
        for g in range(4):
            bk = gbank()
            yield op(PE, [mm(PS[bk][:, 0:Tn], WG[:, g, :], POOLED[:, g, 0:Tn], True, True)],
               reads=[B["pooled"], WGb], writes=[PSB[bk]])
            yield op(DVE, [lambda bk=bk, g=g: V.tensor_scalar(out=PTP[:, g, 0:Tn], in0=PS[bk][:, 0:Tn],
                                                        scalar1=PSC[:, 4 * l + g:4 * l + g + 1], scalar2=None, op0=ALU.mult)],
               reads=[PSB[bk], B["psc"]], writes=[B["ptp"]], disjoint=(g > 0))


    def stage_ATT(blk):
        l = blk["l"]
        Tn = blk["T"]
        tiles = blk["tiles"]
        for sg in blk["segs"]:
            attention(sg["col0"], sg["n"], lambda hp, sg=sg: sg["ktiles"](l, hp))


    def stage_MERGE_MIX(blk):
        l = blk["l"]
        Tn = blk["T"]
        tiles = blk["tiles"]
        SG = [R[:, 0, :].rearrange("p (k t) -> p k t", k=8), R[:, 1, :].rearrange("p (k t) -> p k t", k=8)]
        ATT = R[:, 3, 0:2048].rearrange("p (h t) -> p h t", h=4)
        MT = R[:, 2, :].rearrange("p (k t) -> p k t", k=8)
        for oc in range(8):
            j, cc = oc // 4, (oc % 4) * 128
            if oc % 4 == 0:
                WA, WAb = w_get(l, 10 + 2 * j)
                WP, WPb_ = w_get(l, 11 + 2 * j, hold=1)
            ba = gbank()
            op(PE, [mm(PS[ba][:, 0:Tn], WA[:, kc, cc:cc + 128], ATT[:, kc, 0:Tn], kc == 0, kc == 3) for kc in range(4)],
               reads=[Rb[3], WAb], writes=[PSB[ba]])
            bp = gbank()
            op(PE, [mm(PS[bp][:, 0:Tn], WP[:, kc, cc:cc + 128], PTP[:, kc, 0:Tn], kc == 0, kc == 3) for kc in range(4)],
               reads=[B["ptp"], WPb_], writes=[PSB[bp]])
            m0, m1 = 0, 1
            op(DVE, [lambda ba=ba, oc=oc: V.tensor_tensor(out=MG[:, 0, 0:Tn], in0=PS[ba][:, 0:Tn], in1=SG[0][:, oc, 0:Tn],
                                                          op=ALU.mult)],
               reads=[PSB[ba], Rb[0]], writes=[MGb[0]])
            op(DVE, [lambda bp=bp, oc=oc: V.tensor_tensor(out=MG[:, 1, 0:Tn], in0=PS[bp][:, 0:Tn], in1=SG[1][:, oc, 0:Tn],
                                                          op=ALU.mult)],
               reads=[PSB[bp], Rb[1]], writes=[MGb[1]])
            op(POOL, [lambda oc=oc: G.tensor_tensor(out=MT[:, oc, 0:Tn], in0=MG[:, 0, 0:Tn], in1=MG[:, 1, 0:Tn], op=ALU.add)],
               reads=[MGb[0], MGb[1]], writes=[Rb[2]], disjoint=(oc > 0))

        for j in range(2):
            WO, WOb = w_get(l, 14 + j)
            for tl in tiles:
                t, n, col0 = tl["t"], tl["n"], tl["col0"]
                bk = gbank()
                op(PE, [mm(PS[bk][0:n, 0:512], MT[:, kc, col0:col0 + n], WO[:, kc, :], kc == 0, kc == 7) for kc in range(8)],
                   reads=[Rb[2], WOb], writes=[PSB[bk]])
                op(DVE, [lambda bk=bk, t=t, n=n, j=j: V.scalar_tensor_tensor(
                    out=XT[0:n, t, 512 * j:512 * j + 512], in0=XT[0:n, t, 512 * j:512 * j + 512], scalar=ALPHA,
                    in1=PS[bk][0:n, 0:512], op0=ALU.mult, op1=ALU.add)],
                   reads=[PSB[bk], XTb[t]], writes=[XTb[t]])

    LN_TILE_MAJOR = bool(int(os.environ.get("MK_LNTM", "0")))

    def stage_LN1a(blk):
        if LN_TILE_MAJOR:
            for tl in blk["tiles"]:
                yield from layer_norm_multi([tl], 0, 1, True, None, tl["t"])
        else:
            yield from layer_norm_multi(blk["tiles"], 0, 1, True)

    def stage_LN1b(blk):
        transposes(blk["tiles"], lambda tl: (XBN[0:tl["n"], tl["t"], :], XBNb[tl["t"]]), XTRB, B["xtrb"])

    def stage_MLP(blk):
        l = blk["l"]
        Tn = blk["T"]
        tiles = blk["tiles"]
        for j in range(8):
            WUp, WUpb = w_get(l, 16 + j)
            for cc in range(4):
                fc = 4 * j + cc
                bk = gbank()
                op(PE, [mm(PS[bk][:, 0:Tn], WUp[:, kc, 128 * cc:128 * cc + 128], XTRB[:, kc, 0:Tn], kc == 0, kc == 7)
                        for kc in range(8)], reads=[B["xtrb"], WUpb], writes=[PSB[bk]])
                rl = nxt("mg", 2)
                op(ACT, [lambda bk=bk, rl=rl: A.activation(out=RL[:, rl, 0:Tn], in_=PS[bk][:, 0:Tn], func=AF.Relu)],
                   reads=[PSB[bk]], writes=[RLb[rl]])
                g, k = fc // 8, fc % 8
                op(POOL, [lambda rl=rl, g=g, k=k: G.tensor_tensor(out=R[:, g, 512 * k:512 * k + Tn], in0=RL[:, rl, 0:Tn],
                                                                  in1=RL[:, rl, 0:Tn], op=ALU.mult)],
                   reads=[RLb[rl]], writes=[Rb[g]], disjoint=(k > 0))
        for half in range(2):
            for g in range(4):
                WD, WDb = w_get(l, 24 + 4 * half + g)
                for tl in tiles:
                    t, n, col0 = tl["t"], tl["n"], tl["col0"]
                    op(PE, [mm(PS[4 + t][0:n, 0:512], R[:, g, 512 * k + col0:512 * k + col0 + n], WD[:, k, :],
                               g == 0 and k == 0, g == 3 and k == 7) for k in range(8)],
                       reads=[Rb[g], WDb], writes=[PSB[4 + t]])
            for tl in tiles:
                t, n = tl["t"], tl["n"]
                op(DVE, [lambda t=t, n=n, half=half: V.scalar_tensor_tensor(
                    out=XT[0:n, t, 512 * half:512 * half + 512], in0=XT[0:n, t, 512 * half:512 * half + 512], scalar=ALPHA,
                    in1=PS[4 + t][0:n, 0:512], op0=ALU.mult, op1=ALU.add)],
                   reads=[PSB[4 + t], XTb[t]], writes=[XTb[t]])

    def stage_LN2(blk):
        l = blk["l"]
        Tn = blk["T"]
        tiles = blk["tiles"]
        last_layer = (l == NL - 1)
        def after(k, tl):
            dst = tl["yout"][1 if last_layer else 0]
            if dst is None:
                return None
            return dma(SP, olane(), dst, TMPN[0:tl["n"], k, :], reads=[TMPNb[k]],
                       writes=tl["ybuf"] if not last_layer else [])
        if LN_TILE_MAJOR:
            for tl in tiles:
                yield from layer_norm_multi([tl], 2, 3, False, after, tl["t"])
        else:
            yield from layer_norm_multi(tiles, 2, 3, False, after)

    def main_block(i):
        tiles = []
        for t in range(4):
            r0 = 512 * i + 128 * t
            tiles.append(dict(t=t, col0=128 * t, n=128, vt=t,
                              xsrc=None, xsbuf=[],
                              kout=None, vout=None, lout=None,
                              yout=(XS[r0:r0 + 128, :], y_prompt[r0:r0 + 128, :]), ybuf=[XSb[i]], r0=r0))

        def cumsum(l):
            op(PE, [mm(PS[3][:, 64:96], TRIF, LG[:, 0:4, :].rearrange("p t h -> p (t h)"), True, True),
                    mm(PS[3][:, 96:128], ONESF, LG[:, 0:4, :].rearrange("p t h -> p (t h)"), True, True)],
               reads=[B["lg"], B["const"]], writes=[PSB[3]])
            tot = PS[3][:, 96:128].rearrange("p (t h) -> p t h", h=H)
            loc = PS[3][:, 64:96].rearrange("p (t h) -> p t h", h=H)
            op(DVE, [lambda: V.tensor_copy(out=CBT[:, 0, :], in_=CSM[:, 10, :])], reads=[B["csm"]], writes=[B["cbt"]])
            for t in range(1, 5):
                op(DVE, [lambda t=t: V.tensor_tensor(out=CBT[:, t, :], in0=CBT[:, t - 1, :], in1=tot[:, t - 1, :], op=ALU.add)],
                   reads=[PSB[3], B["cbt"]], writes=[B["cbt"]])
            op(DVE, [lambda: V.tensor_tensor(out=CT[:, 4 * i:4 * i + 4, :], in0=loc, in1=CBT[:, 0:4, :], op=ALU.add)],
               reads=[PSB[3], B["cbt"]], writes=[B["ct"]])
            op(DVE, [lambda: V.tensor_copy(out=CSM[:, 10, :], in_=CBT[:, 4, :])], reads=[B["cbt"]], writes=[B["csm"]])
            nt_ = 4 * i + 4
            op(DVE, [lambda: V.tensor_tensor(out=BIAS[:, 0:nt_, :], in0=CBT[:, 4:5, :].broadcast_to([128, nt_, H]),
                                             in1=CT[:, 0:nt_, :], op=ALU.subtract)],
               reads=[B["cbt"], B["ct"]], writes=[B["bias"]])
            op(DVE, [lambda: V.tensor_tensor(out=BSM[:, 0, :], in0=CBT[:, 4, :], in1=CSM[:, 0, :], op=ALU.subtract)],
               reads=[B["cbt"], B["csm"]], writes=[B["bsm"]])
            op(ACT, [lambda: A.activation(out=BIAS[:, 0:nt_, :], in_=BIAS[:, 0:nt_, :], func=AF.Exp)],
               reads=[B["bias"]], writes=[B["bias"]])
            op(ACT, [lambda: A.activation(out=BSM[0:16, 0, :], in_=BSM[0:16, 0, :], func=AF.Exp)],
               reads=[B["bsm"]], writes=[B["bsm"]])

        def publish(l):
            dma(SP, mlane(), KVS[l, i], KVST[:], reads=[B["kvst"]], writes=[KVSb[l][i]])
            published.add((l, i))

        def ktiles(l, hp):
            res = []
            hA, hB = 2 * hp, 2 * hp + 1
            vp = hp % 2
            vbuf = B["vms%d" % vp]
            scale_slab(VMS[:, vp, :], VM[:, hp, :], 16, BSM[0:16, 0, hA:hA + 1], BSM[0:16, 0, hB:hB + 1],
                       [B["vm"], B["bsm"]], vbuf)
            res.append(dict(K=KM[:, hp, :], V=VMS[:, vp, :], nk=16, qlo=0, diag=False, bufs=[B["km"], vbuf]))
            for kb in range(i):
                shared = {}
                for tt in range(4):
                    def mk(tt=tt, kb=kb, shared=shared):
                        if "s" not in shared:
                            s_ = ring_get("main", l, i, hp, kb)
                            shared["s"] = s_
                            scale_slab4(RING[:, s_, 512:512 + 4 * SLAB], BIAS[:, 4 * kb:4 * kb + 4, :], hA, hB,
                                        [RINGb[s_], B["bias"]], RINGb[s_])
                        s_ = shared["s"]
                        vb = 512 + SLAB * tt
                        return dict(K=RING[:, s_, 128 * tt:128 * tt + 128], V=RING[:, s_, vb:vb + SLAB],
                                    nk=128, qlo=0, diag=False, bufs=[RINGb[s_]])
                    res.append(mk)
            scale_slab4(KVST[:, hp, 512:512 + 4 * SLAB], BIAS[:, 4 * i:4 * i + 4, :], hA, hB,
                        [B["kvst"], B["bias"]], B["kvst"])
            for tt in range(4):
                g = 4 * i + tt
                vb = 512 + SLAB * tt
                res.append(dict(K=KVST[:, hp, 128 * tt:128 * tt + 128], V=KVST[:, hp, vb:vb + SLAB],
                                nk=128, qlo=128 * tt, diag=True, bufs=[B["kvst"]]))
            return res

        def post(l):
            op(POOL, [lambda: G.tensor_copy(out=UB[:, :, 0:16], in_=UB[:, :, 512:528])], reads=[B["ub"]], writes=[B["ub"]])

        def post_last(l):
            pool_out(l, UB, B["ub"], 513, pool_prompt)
            post(l)

        seg = dict(kind="main", col0=0, n=512, U=UB, Ubuf=B["ub"], ktiles=ktiles,
                   post=post_last if i == NBLK - 1 else post)
        return dict(T=512, tiles=tiles, segs=[seg], cumsum=cumsum, publish=publish, i=i)

    def pool_out(l, ub, ubuf, c_lo, dst):
        bk = gbank()
        op(PE, [(lambda g=g: T.transpose(out=PS[bk][0:15, 128 * g:128 * g + 128], in_=ub[:, g, c_lo:c_lo + 15],
                                         identity=IDF)) for g in range(4)],
           reads=[ubuf, B["const"]], writes=[PSB[bk]])
        op(DVE, [lambda: V.tensor_copy(out=POUT[0:15, :], in_=PS[bk][0:15, 0:512])], reads=[PSB[bk]], writes=[B["pout"]])
        dma(SP, olane(), dst[l], POUT[0:15, :], reads=[B["pout"]])

    def mini_block():
        tiles = [dict(t=0, col0=0, n=16, vt=0, xsrc=None, xsbuf=[], kout=None, vout=None, lout=None,
                      yout=(XS[SEQ:SEQ + 16, :], None), ybuf=[XSb[NBLK]]),
                 dict(t=1, col0=16, n=32, vt=1, xsrc=None, xsbuf=[], kout=None, vout=None, lout=None,
                      yout=(XS[SEQ + 16:SEQ + 48, :], y_sample[:, :]), ybuf=[XSb[NBLK]])]

        def cumsum(l):
            op(PE, [mm(PS[3][0:16, 64:72], TRIF[0:16, 0:16], LG[0:16, 0, :], True, True),
                    mm(PS[3][:, 72:80], ONESF[0:16, :], LG[0:16, 0, :], True, True)],
               reads=[B["lg"], B["const"]], writes=[PSB[3]])
            op(DVE, [lambda: V.tensor_copy(out=CSM[0:16, 0, :], in_=PS[3][0:16, 64:72])], reads=[PSB[3]], writes=[B["csm"]])
            op(DVE, [lambda: V.tensor_copy(out=CSM[:, 10, :], in_=PS[3][:, 72:80])], reads=[PSB[3]], writes=[B["csm"]])
            op(DVE, [lambda: V.tensor_tensor(out=BSM[0:16, 3, :], in0=CSM[0:16, 10, :], in1=CSM[0:16, 0, :], op=ALU.subtract)],
               reads=[B["csm"]], writes=[B["bsm"]])
            with nc.allow_non_contiguous_dma(reason="cache logf rows are 32B"):
                dma(SP, mlane(), LGT[:, 0:8, :], cache_logf[l].rearrange("(t p) h -> p t h", p=128), writes=[B["lgt"]])
            op(PE, [mm(PS[3][:, 128:192], TRIF, LGT[:, 0:8, :].rearrange("p t h -> p (t h)"), True, True),
                    mm(PS[3][:, 192:256], ONESF, LGT[:, 0:8, :].rearrange("p t h -> p (t h)"), True, True),
                    mm(PS[3][0:32, 256:264], TRIF[0:32, 0:32], LG[0:32, 1, :], True, True),
                    mm(PS[3][:, 264:272], ONESF[0:32, :], LG[0:32, 1, :], True, True)],
               reads=[B["lg"], B["lgt"], B["const"]], writes=[PSB[3]])
            tot = PS[3][:, 192:256].rearrange("p (t h) -> p t h", h=H)
            loc = PS[3][:, 128:192].rearrange("p (t h) -> p t h", h=H)
            op(DVE, [lambda: V.memset(CBT[:, 0, :], 0.0)], writes=[B["cbt"]])
            for t in range(1, 9):
                op(DVE, [lambda t=t: V.tensor_tensor(out=CBT[:, t, :], in0=CBT[:, t - 1, :], in1=tot[:, t - 1, :], op=ALU.add)],
                   reads=[PSB[3], B["cbt"]], writes=[B["cbt"]])
            op(DVE, [lambda: V.tensor_tensor(out=CSM[:, 2:10, :], in0=loc, in1=CBT[:, 0:8, :], op=ALU.add)],
               reads=[PSB[3], B["cbt"]], writes=[B["csm"]])
            op(DVE, [lambda: V.tensor_tensor(out=CSM[0:32, 1, :], in0=PS[3][0:32, 256:264], in1=CBT[0:32, 8, :], op=ALU.add)],
               reads=[PSB[3], B["cbt"]], writes=[B["csm"]])
            op(DVE, [lambda: V.tensor_tensor(out=CSM[:, 11, :], in0=PS[3][:, 264:272], in1=CBT[:, 8, :], op=ALU.add)],
               reads=[PSB[3], B["cbt"]], writes=[B["csm"]])
            op(DVE, [lambda: V.tensor_tensor(out=BSM[:, 4:12, :], in0=CSM[:, 11:12, :].broadcast_to([128, 8, H]),
                                             in1=CSM[:, 2:10, :], op=ALU.subtract)],
               reads=[B["csm"]], writes=[B["bsm"]])
            op(DVE, [lambda: V.tensor_tensor(out=BSM[0:32, 1, :], in0=CSM[0:32, 11, :], in1=CSM[0:32, 1, :], op=ALU.subtract)],
               reads=[B["csm"]], writes=[B["bsm"]])
            op(ACT, [lambda: A.activation(out=BSM[0:16, 3, :], in_=BSM[0:16, 3, :], func=AF.Exp),
                     lambda: A.activation(out=BSM[:, 4:12, :], in_=BSM[:, 4:12, :], func=AF.Exp),
                     lambda: A.activation(out=BSM[0:32, 1, :], in_=BSM[0:32, 1, :], func=AF.Exp)],
               reads=[B["bsm"]], writes=[B["bsm"]])

        def publish(l):
            op(POOL, [lambda: G.tensor_copy(out=KM[:, :, :], in_=KVST[:, :, 0:16])], reads=[B["kvst"]], writes=[B["km"]])
            op(POOL, [lambda: G.tensor_copy(out=VM[0:16, :, :], in_=KVST[0:16, :, 512:512 + SLAB])],
               reads=[B["kvst"]], writes=[B["vm"]])

        def ktiles_meta(l, hp):
            hA, hB = 2 * hp, 2 * hp + 1
            vp = hp % 2
            vbuf = B["vms%d" % vp]
            scale_slab(VMS[:, vp, :], VM[:, hp, :], 16, BSM[0:16, 3, hA:hA + 1], BSM[0:16, 3, hB:hB + 1],
                       [B["vm"], B["bsm"]], vbuf)
            return [dict(K=KM[:, hp, :], V=VMS[:, vp, :], nk=16, qlo=0, diag=True, bufs=[B["km"], vbuf])]

        def ktiles_sample(l, hp):
            res = []
            CKB = R[:, 2, :].rearrange("p (t c) -> p t c", t=8)
            for kb in range(2):
                s = ring_get("sample", l, -1, hp, kb)
                bk = gbank()
                pb = PS[bk][:].bitcast(BF16)
                op(PE, [(lambda tt=tt, kb=kb, pb=pb: T.transpose(out=pb[:, 128 * tt:128 * tt + 128],
                                                               in_=CKB[:, 4 * kb + tt, 128 * hp:128 * hp + 128],
                                                               identity=IDB)) for tt in range(4)],
                   reads=[Rb[2], B["cb16"]], writes=[PSB[bk]])
                op(DVE, [lambda s=s, pb=pb: V.tensor_copy(out=RING[:, s, 0:512], in_=pb[:, 0:512])],
                   reads=[PSB[bk]], writes=[RINGb[s]])
                for tt in range(4):
                    op(POOL, [lambda tt=tt: G.memset(RING[:, s, 512 + SLAB * tt + 64:512 + SLAB * tt + 98], 1.0)],
                       writes=[RINGb[s]], disjoint=True)
                for tt in range(4):
                    r0 = 512 * kb + 128 * tt
                    src = cache_v[l][r0:r0 + 128, 128 * hp:128 * hp + 128].rearrange("p (b c) -> p b c", b=2)
                    for hh in range(2):
                        base = 512 + SLAB * tt + 98 * hh
                        dma(POOL, CLn[nxt("c", 8)], RING[:, s, base:base + 64], src[:, hh, :], writes=[RINGb[s]], disjoint=True)
                scale_slab4(RING[:, s, 512:512 + 4 * SLAB], BSM[:, 4 + 4 * kb:8 + 4 * kb, :], 2 * hp, 2 * hp + 1,
                            [RINGb[s], B["bsm"]], RINGb[s])
                for tt in range(4):
                    g = 4 * kb + tt
                    vb = 512 + SLAB * tt
                    res.append(dict(K=RING[:, s, 128 * tt:128 * tt + 128], V=RING[:, s, vb:vb + SLAB],
                                    nk=128, qlo=0, diag=False, bufs=[RINGb[s]]))
            vb = 512 + SLAB
            scale_slab(KVST[:, hp, vb:vb + SLAB], KVST[:, hp, vb:vb + SLAB], 32,
                       BSM[0:32, 1, 2 * hp:2 * hp + 1], BSM[0:32, 1, 2 * hp + 1:2 * hp + 2], [B["kvst"], B["bsm"]], B["kvst"])
            res.append(dict(K=KVST[:, hp, 16:48], V=KVST[:, hp, vb:vb + SLAB], nk=32, qlo=0, diag=True, bufs=[B["kvst"]]))
            return res

        def pre_sample(l):
            dma(SP, mlane(), SPT[0:15, :], state_pool[l], writes=[B["pout"]])
            bk = gbank()
            op(PE, [(lambda g=g: T.transpose(out=PS[bk][:, 16 * g:16 * g + 15], in_=SPT[0:15, 128 * g:128 * g + 128],
                                             identity=IDF[0:15, 0:15])) for g in range(4)],
               reads=[B["pout"], B["const"]], writes=[PSB[bk]])
            op(DVE, [lambda: V.tensor_copy(out=US[:, :, 1:16], in_=PS[bk][:, 0:64].rearrange("p (g c) -> p g c", g=4)[:, :, 0:15])],
               reads=[PSB[bk]], writes=[B["us"]])

        def post_sample(l):
            pool_out(l, US, B["us"], 33, pool_sample)

        def post_meta(l):
            op(POOL, [lambda: G.tensor_copy(out=UB[:, :, 0:16], in_=UM[:, :, 16:32])], reads=[B["um"]], writes=[B["ub"]])

        segs = [dict(kind="meta", col0=0, n=16, U=UM, Ubuf=B["um"], ktiles=ktiles_meta, post=post_meta),
                dict(kind="sample", col0=16, n=32, U=US, Ubuf=B["us"], ktiles=ktiles_sample, pre=pre_sample, post=post_sample)]
        return dict(T=48, tiles=tiles, segs=segs, cumsum=cumsum, publish=publish, i=-1)

    blocks = []
    for l in range(NL):
        mb = mini_block()
        mb["l"] = l
        mb["first"] = True
        mt, st = mb["tiles"]
        if l == 0:
            mt["xsrc"], st["xsrc"] = meta_tokens, x_sample
        else:
            mt["xsrc"], st["xsrc"] = XS[SEQ:SEQ + 16, :], XS[SEQ + 16:SEQ + 48, :]
            mt["xsbuf"] = st["xsbuf"] = [XSb[NBLK]]
        mt["kout"], mt["vout"], mt["lout"] = k_prompt[l, 0:16, :], v_prompt[l, 0:16, :], logf_prompt[l, 0:16, :]
        st["kout"], st["vout"], st["lout"] = k_sample[l], v_sample[l], logf_sample[l]
        blocks.append(mb)
        for i in range(NB):
            blk = main_block(i)
            blk["l"] = l
            blk["first"] = False
            for tl in blk["tiles"]:
                r0 = tl["r0"]
                if l == 0:
                    tl["xsrc"] = x_prompt[r0:r0 + 128, :]
                else:
                    tl["xsrc"] = XS[r0:r0 + 128, :]
                    tl["xsbuf"] = [XSb[i]]
                tl["kout"] = k_prompt[l, 16 + r0:16 + r0 + 128, :]
                tl["vout"] = v_prompt[l, 16 + r0:16 + r0 + 128, :]
                tl["lout"] = logf_prompt[l, 16 + r0:16 + r0 + 128, :]
            blocks.append(blk)

    P1C, P2C, MMC, MLPC = [0, 1, 2, 3], [4, 5, 6, 7, 8, 9], [10, 11, 12, 13, 14, 15], list(range(16, 32))
    wseq.clear()
    wseq.extend((blocks[0]["l"], c) for c in P1C)
    for k, b in enumerate(blocks):
        nb_ = blocks[k + 1] if k + 1 < len(blocks) else None
        wseq.extend((b["l"], c) for c in P2C + MMC)
        if nb_ is not None:
            wseq.extend((nb_["l"], c) for c in P1C)
        wseq.extend((b["l"], c) for c in MLPC)

    def drive(*gens):
        gens = [g for g in gens if g is not None]
        while gens:
            for g in list(gens):
                try:
                    next(g)
                except StopIteration:
                    gens.remove(g)

    stage_xbf(blocks[0])
    emit_casts(0)
    stage_xT(blocks[0])
    drive(stage_P1(blocks[0]))
    if len(blocks) > 1:
        stage_xbf(blocks[1])
    prev = None
    for k, b in enumerate(blocks):
        nb_ = blocks[k + 1] if k + 1 < len(blocks) else None
        nnb_ = blocks[k + 2] if k + 2 < len(blocks) else None
        l = b["l"]
        drive(stage_LN2(prev) if prev is not None else None, stage_P2(b))
        if b["first"]:
            for kk, src in enumerate([ln1_g, ln1_b, ln2_g, ln2_b]):
                dma(SP, mlane(), LNP[:, kk, :], src[l].partition_broadcast(128), writes=[B["lnp"]], disjoint=(kk > 0))
            dma(POOL, CLn[nxt("c", 8)], R[:, 2, :].rearrange("p (t c) -> p t c", t=8),
                cache_k[l].rearrange("(t p) c -> p t c", p=128), writes=[Rb[2]])
        stage_xload(b)
        if nb_ is not None:
            stage_xT(nb_)
        if k == 0 and NL > 1:
            emit_casts(1)
        stage_ATT(b)
        stage_MERGE_MIX(b)
        drive(stage_LN1a(b), stage_P1(nb_) if nb_ is not None else None)
        stage_LN1b(b)
        if nnb_ is not None:
            stage_xbf(nnb_)
        stage_MLP(b)
        prev = b
    drive(stage_LN2(prev))

    for ln in OL + ML + XL + XBL + WL + RLn + CLn:
        SP.wait(ln.last)
    for e in (PE, ACT, DVE, POOL):
        if e.count:
            SP.eng.wait_ge(e.sem, e.count)
    print("instr counts:", {e.name: e.count for e in (PE, ACT, DVE, POOL)}, "waits:",
          {e.name: e.nwaits for e in (PE, ACT, DVE, POOL, SP)})
    return nc


def make_consts():
    c = np.zeros((128, 576), np.float32)
    c[64, 448:512] = 1.0
    c[32, 512:576] = 1.0
    s = np.arange(128)
    c[:, 0:128] = (s[:, None] <= s[None, :]).astype(np.float32)
    c[:, 128:256] = 1.0
    c[:, 256:384] = np.eye(128, dtype=np.float32)
    for g, w in enumerate((2, 4, 8, 16)):
        for p in range(16):
            c[:, 384 + 16 * g + p] = 1.0 / min(w, p + 1)
    return c


_NC_CACHE = {}


def kernel(x_prompt, x_sample, cache_k, cache_v, cache_logf, state_pool, meta_tokens,
           w_in, b_f, w_pool_grp, pool_scale, w_attn_br, w_pool_br, w_out,
           ln1_g, ln1_b, w_up, w_down, ln2_g, ln2_b):
    NB = int(os.environ.get("MK_NB", NBLK))
    NL = int(os.environ.get("MK_NL", DEPTH))
    f = lambda a: np.ascontiguousarray(np.asarray(a, dtype=np.float32))
    nc = build(NB, NL)
    consts = make_consts()
    shared = dict(meta_tokens=f(meta_tokens), w_in=f(w_in), b_f=f(b_f), w_pool_grp=f(w_pool_grp),
                  pool_scale=f(pool_scale), w_attn_br=f(w_attn_br), w_pool_br=f(w_pool_br), w_out=f(w_out),
                  ln1_g=f(ln1_g), ln1_b=f(ln1_b), w_up=f(w_up), w_down=f(w_down), ln2_g=f(ln2_g), ln2_b=f(ln2_b),
                  consts=consts)
    in_maps = []
    for b in range(8):
        m = dict(shared)
        m["x_prompt"] = f(x_prompt[b])
        m["x_sample"] = f(x_sample[b])
        m["cache_k"] = f(np.asarray(cache_k)[:, b].reshape(DEPTH, PAST, DA))
        m["cache_v"] = f(np.asarray(cache_v)[:, b].reshape(DEPTH, PAST, DA))
        m["cache_logf"] = f(np.asarray(cache_logf)[:, b])
        m["state_pool"] = f(np.asarray(state_pool)[:, b])
        in_maps.append(m)
    res = run_bass_kernel_spmd(nc, in_maps, core_ids=list(range(8)))
    rs = res.results
    st = lambda name, ax: np.stack([np.asarray(r[name]) for r in rs], axis=ax)
    y_prompt = st("y_prompt", 0)
    y_sample = st("y_sample", 0)
    k_prompt = st("k_prompt", 1).reshape(DEPTH, 8, L_TOT, H, DH)
    v_prompt = st("v_prompt", 1).reshape(DEPTH, 8, L_TOT, H, DH)
    logf_prompt = st("logf_prompt", 1)
    pool_prompt = st("pool_prompt", 1)
    k_sample = st("k_sample", 1).reshape(DEPTH, 8, TS, H, DH)
    v_sample = st("v_sample", 1).reshape(DEPTH, 8, TS, H, DH)
    logf_sample = st("logf_sample", 1)
    pool_sample = st("pool_sample", 1)
    return (y_prompt, y_sample, k_prompt, v_prompt, logf_prompt, pool_prompt,
            k_sample, v_sample, logf_sample, pool_sample)
```
